# Optimizing a Trainium2 kernel written in Bass

```python
import math
import jax, jax.numpy as jnp
from jax import lax
import numpy as np

D_MODEL = 4096
BATCH = 1
SEQ = 16384
DEPTH = 4

N_META = 16
N_MIXERS = 2
N_MLA_LAYERS = (DEPTH + 1) // 2
N_GDN_LAYERS = DEPTH // 2
NORM_EPS = 1e-6

MLA_HEADS = D_MODEL // 64
MLA_Q_RANK = 1536
MLA_KV_RANK = 512
MLA_NOPE_DIM = 128
MLA_ROPE_DIM = 64
MLA_V_DIM = 128
MLA_QK_DIM = MLA_NOPE_DIM + MLA_ROPE_DIM
ROPE_THETA = 10000.0
Q_BLOCK = 128

GDN_HEADS = D_MODEL // 128
GDN_K_DIM = 128
GDN_V_DIM = 128
GDN_KEY_WIDTH = GDN_HEADS * GDN_K_DIM
GDN_VALUE_WIDTH = GDN_HEADS * GDN_V_DIM
GDN_CONV = 4
GDN_CHUNK = 64

D_FF = 4 * D_MODEL

kernel_name = "hybrid_mla_gdn_sqrelu_sandwich_meta"


def _rmsnorm(x, w):
    x32 = x.astype(jnp.float32)
    y = x32 * lax.rsqrt(jnp.mean(x32 * x32, axis=-1, keepdims=True) + NORM_EPS)
    return (y * w.astype(jnp.float32)).astype(x.dtype)


def _l2norm(t):
    return t * lax.rsqrt(jnp.sum(t * t, axis=-1, keepdims=True) + NORM_EPS)


def _rope_tables(length):
    inv = ROPE_THETA ** (-jnp.arange(0, MLA_ROPE_DIM, 2, dtype=jnp.float32) / MLA_ROPE_DIM)
    ang = jnp.arange(length, dtype=jnp.float32)[:, None] * inv[None, :]
    return jnp.cos(ang), jnp.sin(ang)


def _apply_rope(x, cos, sin):
    x32 = x.astype(jnp.float32)
    half = MLA_ROPE_DIM // 2
    x1, x2 = x32[..., :half], x32[..., half:]
    c, s = cos[:, None, :], sin[:, None, :]
    return jnp.concatenate([x1 * c - x2 * s, x2 * c + x1 * s], axis=-1).astype(x.dtype)


def _mla(h, wq_a, q_norm, wq_b, wkv_a, kv_norm, wkv_b, wo, cos, sin):
    b, L, _ = h.shape
    cq = _rmsnorm(h @ wq_a, q_norm)
    q = (cq @ wq_b).reshape(b, L, MLA_HEADS, MLA_QK_DIM)
    q_nope = q[..., :MLA_NOPE_DIM]
    q_pe = _apply_rope(q[..., MLA_NOPE_DIM:], cos, sin)
    kv_a = h @ wkv_a
    c_kv = _rmsnorm(kv_a[..., :MLA_KV_RANK], kv_norm)
    k_pe = _apply_rope(kv_a[..., None, MLA_KV_RANK:], cos, sin)[:, :, 0]
    kv = (c_kv @ wkv_b).reshape(b, L, MLA_HEADS, MLA_NOPE_DIM + MLA_V_DIM)
    k_nope, v = kv[..., :MLA_NOPE_DIM], kv[..., MLA_NOPE_DIM:]

    n_blocks = -(-L // Q_BLOCK)
    pad = n_blocks * Q_BLOCK - L

    def to_blocks(t):
        t = jnp.pad(t, ((0, 0), (0, pad), (0, 0), (0, 0)))
        return jnp.moveaxis(t.reshape(b, n_blocks, Q_BLOCK, *t.shape[2:]), 1, 0)

    starts = jnp.arange(n_blocks, dtype=jnp.int32) * Q_BLOCK
    key_pos = jnp.arange(L, dtype=jnp.int32)
    scale = MLA_QK_DIM ** -0.5

    def attend(args):
        qn, qp, start = args
        s = (jnp.einsum('bqhd,bkhd->bhqk', qn, k_nope)
             + jnp.einsum('bqhr,bkr->bhqk', qp, k_pe)).astype(jnp.float32) * scale
        q_pos = start + jnp.arange(Q_BLOCK, dtype=jnp.int32)
        s = jnp.where(key_pos[None, :] <= q_pos[:, None], s, -jnp.inf)
        p = jax.nn.softmax(s, axis=-1).astype(v.dtype)
        return jnp.einsum('bhqk,bkhd->bqhd', p, v)

    o = lax.map(attend, (to_blocks(q_nope), to_blocks(q_pe), starts))
    o = jnp.moveaxis(o, 0, 1).reshape(b, n_blocks * Q_BLOCK, MLA_HEADS * MLA_V_DIM)[:, :L]
    return o @ wo


def _gated_delta_rule(q, k, v, g, beta):
    b, L, H, dk = q.shape
    dv = v.shape[-1]
    C = GDN_CHUNK
    lead = C - N_META

    def front_pad(t):
        return jnp.pad(t, ((0, 0), (lead, 0)) + ((0, 0),) * (t.ndim - 2))

    q, k, v, g, beta = (front_pad(t) for t in (q, k, v, g, beta))
    n = (L + lead) // C

    def chunks(t):
        return jnp.moveaxis(t.reshape(b, n, C, H, *t.shape[3:]), 3, 1)

    q, k, v, g, beta = (chunks(t) for t in (q, k, v, g, beta))
    gc = jnp.cumsum(g, axis=-1)
    idx = jnp.arange(C)
    causal = idx[:, None] >= idx[None, :]
    decay = jnp.exp(jnp.where(causal, gc[..., :, None] - gc[..., None, :], -jnp.inf))
    kb = k * beta[..., None]
    a_mat = jnp.einsum('bhncd,bhnsd->bhncs', kb, k) * decay
    rhs = jnp.concatenate([v * beta[..., None], kb * jnp.exp(gc)[..., None]], axis=-1)
    uw = lax.linalg.triangular_solve(a_mat, rhs, left_side=True, lower=True, unit_diagonal=True)
    u, w = uw[..., :dv], uw[..., dv:]
    attn_intra = jnp.einsum('bhncd,bhnsd->bhncs', q, k) * decay
    q_dec = q * jnp.exp(gc)[..., None]
    g_last = gc[..., -1]
    k_dec = k * jnp.exp(g_last[..., None] - gc)[..., None]

    def step(S, xs):
        u_c, w_c, qd_c, kd_c, at_c, gl_c = xs
        v_new = u_c - jnp.einsum('bhcd,bhde->bhce', w_c, S)
        o_c = jnp.einsum('bhcd,bhde->bhce', qd_c, S) + jnp.einsum('bhcs,bhse->bhce', at_c, v_new)
        S = S * jnp.exp(gl_c)[..., None, None] + jnp.einsum('bhcd,bhce->bhde', kd_c, v_new)
        return S, o_c

    xs = tuple(jnp.moveaxis(t, 2, 0) for t in (u, w, q_dec, k_dec, attn_intra, g_last))
    S0 = jnp.zeros((b, H, dk, dv), jnp.float32)
    _, o = lax.scan(step, S0, xs)
    o = jnp.moveaxis(jnp.moveaxis(o, 0, 2), 1, 3).reshape(b, n * C, H, dv)
    return o[:, lead:]


def _gdn(h, w_qkvz, w_ba, conv_w, a_log, dt_bias, o_norm, wo):
    b, L, _ = h.shape
    KW, VW = GDN_KEY_WIDTH, GDN_VALUE_WIDTH
    proj = h @ w_qkvz
    qkv = proj[..., :2 * KW + VW]
    z = proj[..., 2 * KW + VW:]
    qkv = lax.conv_general_dilated(qkv, conv_w[:, None, :].astype(qkv.dtype), window_strides=(1,),
                                   padding=((GDN_CONV - 1, 0),),
                                   dimension_numbers=('NWC', 'WIO', 'NWC'),
                                   feature_group_count=qkv.shape[-1])
    qkv = jax.nn.silu(qkv).astype(jnp.float32)
    q = _l2norm(qkv[..., :KW].reshape(b, L, GDN_HEADS, GDN_K_DIM)) * (GDN_K_DIM ** -0.5)
    k = _l2norm(qkv[..., KW:2 * KW].reshape(b, L, GDN_HEADS, GDN_K_DIM))
    v = qkv[..., 2 * KW:].reshape(b, L, GDN_HEADS, GDN_V_DIM)
    ba = (h @ w_ba).astype(jnp.float32)
    beta = jax.nn.sigmoid(ba[..., :GDN_HEADS])
    g = -jnp.exp(a_log.astype(jnp.float32)) * jax.nn.softplus(ba[..., GDN_HEADS:] + dt_bias.astype(jnp.float32))
    o = _gated_delta_rule(q, k, v, g, beta)
    o = _rmsnorm(o, o_norm) * jax.nn.silu(z.reshape(b, L, GDN_HEADS, GDN_V_DIM).astype(jnp.float32))
    return o.reshape(b, L, VW).astype(h.dtype) @ wo


def _sqrelu_mlp(h, w_up, w_down):
    return jnp.square(jax.nn.relu(h @ w_up)) @ w_down


def _dense(key, shape, fan_in):
    return jax.random.normal(key, shape, jnp.float32) * (fan_in ** -0.5)


def _gain(key, shape):
    return 1.0 + 0.02 * jax.random.normal(key, shape, jnp.float32)


def setup_inputs(seed: int = 0) -> dict:
    key = jax.random.key(seed)
    ks = jax.random.split(key, 20)
    KW, VW = GDN_KEY_WIDTH, GDN_VALUE_WIDTH
    x = jax.random.normal(ks[0], (BATCH, SEQ, D_MODEL), jnp.float32)
    meta_tokens = jax.random.normal(ks[1], (N_META, D_MODEL), jnp.float32)
    norm_gains = _gain(ks[2], (DEPTH, 4, D_MODEL))
    mla_wq_a = _dense(ks[3], (N_MLA_LAYERS, D_MODEL, MLA_Q_RANK), D_MODEL)
    mla_q_norm = _gain(ks[4], (N_MLA_LAYERS, MLA_Q_RANK))
    mla_wq_b = _dense(ks[5], (N_MLA_LAYERS, MLA_Q_RANK, MLA_HEADS * MLA_QK_DIM), MLA_Q_RANK)
    mla_wkv_a = _dense(ks[6], (N_MLA_LAYERS, D_MODEL, MLA_KV_RANK + MLA_ROPE_DIM), D_MODEL)
    mla_kv_norm = _gain(ks[7], (N_MLA_LAYERS, MLA_KV_RANK))
    mla_wkv_b = _dense(ks[8], (N_MLA_LAYERS, MLA_KV_RANK, MLA_HEADS * (MLA_NOPE_DIM + MLA_V_DIM)), MLA_KV_RANK)
    mla_wo = _dense(ks[9], (N_MLA_LAYERS, MLA_HEADS * MLA_V_DIM, D_MODEL), MLA_HEADS * MLA_V_DIM)
    gdn_w_qkvz = _dense(ks[10], (N_GDN_LAYERS, D_MODEL, 2 * KW + 2 * VW), D_MODEL)
    gdn_w_ba = _dense(ks[11], (N_GDN_LAYERS, D_MODEL, 2 * GDN_HEADS), D_MODEL)
    gdn_conv_w = _dense(ks[12], (N_GDN_LAYERS, GDN_CONV, 2 * KW + VW), GDN_CONV)
    gdn_a_log = jnp.log(jax.random.uniform(ks[13], (N_GDN_LAYERS, GDN_HEADS), jnp.float32, 1.0, 16.0))
    dt = jnp.exp(jax.random.uniform(ks[14], (N_GDN_LAYERS, GDN_HEADS), jnp.float32,
                                    math.log(1e-3), math.log(1e-1)))
    gdn_dt_bias = dt + jnp.log(-jnp.expm1(-dt))
    gdn_o_norm = _gain(ks[15], (N_GDN_LAYERS, GDN_V_DIM))
    gdn_wo = _dense(ks[16], (N_GDN_LAYERS, VW, D_MODEL), VW)
    mlp_w_up = _dense(ks[17], (DEPTH, D_MODEL, D_FF), D_MODEL)
    mlp_w_down = _dense(ks[18], (DEPTH, D_FF, D_MODEL), D_FF)
    return {"x": x, "meta_tokens": meta_tokens, "norm_gains": norm_gains,
            "mla_wq_a": mla_wq_a, "mla_q_norm": mla_q_norm, "mla_wq_b": mla_wq_b,
            "mla_wkv_a": mla_wkv_a, "mla_kv_norm": mla_kv_norm, "mla_wkv_b": mla_wkv_b,
            "mla_wo": mla_wo, "gdn_w_qkvz": gdn_w_qkvz, "gdn_w_ba": gdn_w_ba,
            "gdn_conv_w": gdn_conv_w, "gdn_a_log": gdn_a_log, "gdn_dt_bias": gdn_dt_bias,
            "gdn_o_norm": gdn_o_norm, "gdn_wo": gdn_wo,
            "mlp_w_up": mlp_w_up, "mlp_w_down": mlp_w_down}


def reference(x, meta_tokens, norm_gains, mla_wq_a, mla_q_norm, mla_wq_b, mla_wkv_a, mla_kv_norm,
              mla_wkv_b, mla_wo, gdn_w_qkvz, gdn_w_ba, gdn_conv_w, gdn_a_log, gdn_dt_bias,
              gdn_o_norm, gdn_wo, mlp_w_up, mlp_w_down):
    b = x.shape[0]
    meta = jnp.broadcast_to(meta_tokens.astype(x.dtype)[None], (b, N_META, D_MODEL))
    h = jnp.concatenate([meta, x], axis=1)
    cos, sin = _rope_tables(h.shape[1])
    for i in range(DEPTH):
        gains = norm_gains[i]
        hn = _rmsnorm(h, gains[0])
        j = i // N_MIXERS
        if i % N_MIXERS == 0:
            mix = _mla(hn, mla_wq_a[j], mla_q_norm[j], mla_wq_b[j], mla_wkv_a[j], mla_kv_norm[j],
                       mla_wkv_b[j], mla_wo[j], cos, sin)
        else:
            mix = _gdn(hn, gdn_w_qkvz[j], gdn_w_ba[j], gdn_conv_w[j], gdn_a_log[j], gdn_dt_bias[j],
                       gdn_o_norm[j], gdn_wo[j])
        h = h + _rmsnorm(mix, gains[1])
        ff = _sqrelu_mlp(_rmsnorm(h, gains[2]), mlp_w_up[i], mlp_w_down[i])
        h = h + _rmsnorm(ff, gains[3])
    return h[:, N_META:]
```

```python
import math
import os
from contextlib import ExitStack
import numpy as np
import concourse.bass as bass
import concourse.mybir as mybir
from concourse.bass_utils import run_bass_kernel_spmd

F32 = mybir.dt.float32
BF16 = mybir.dt.bfloat16
AF = mybir.ActivationFunctionType
ALU = mybir.AluOpType
AX = mybir.AxisListType
NCORES = 8
NM = 16
EPS = 1e-6


class Cfg:
    def __init__(self, D=4096, SEQ=16384, DEPTH=4, QR=1536, KVR=512, FF=None):
        self.D, self.SEQ, self.DEPTH, self.QR, self.KVR = D, SEQ, DEPTH, QR, KVR
        self.FF = FF or 4 * D
        self.HM = D // 64
        self.HG = D // 128
        self.HMC = self.HM // NCORES
        self.HGC = self.HG // NCORES
        self.Tc = SEQ // NCORES
        self.TcM = self.Tc + NM
        self.T = SEQ + NM
        self.KC = D // 128
        self.NT = max(1, self.TcM // 258) if self.TcM % 258 == 0 else 1
        self.TT = self.TcM // self.NT
        assert self.TT <= 512
        self.SQ = min(512, self.Tc)
        self.NLAT = QR + KVR + 128
        self.NA = self.NLAT // 128


class T:
    __slots__ = ("ap", "lw", "rd", "ch", "psum")

    def __init__(self, ap, psum=False):
        self.ap = ap
        self.lw = None
        self.rd = {}
        self.psum = psum

    def __getitem__(self, k):
        return self.ap[k]


class DmaCh:
    def __init__(self, sem):
        self.sem = sem
        self.cnt = 0


class Sched:
    def __init__(self, nc, engines, sems):
        self.nc = nc
        self.h = engines
        self.sem = sems
        self.cnt = {n: 0 for n in engines}
        self.seen = {n: {} for n in engines}
        self.chs = []
        self.nins = 0
        self.epoch = 0

    def new_epoch(self, sems):
        self.barrier()
        self.sem = sems
        self.cnt = {n: 0 for n in self.h}
        self.epoch += 1

    def _wait(self, e, tk):
        if tk is None:
            return
        sem, val, src = tk[0], tk[1], tk[2]
        if src is not None and tk[3] < self.epoch:
            return
        if src == e and e == "pe":
            return
        sid = id(sem)
        if self.seen[e].get(sid, 0) >= val:
            return
        self.h[e].wait_ge(sem, val)
        self.seen[e][sid] = val
        self.nins += 1

    def deps(self, e, reads, writes):
        for t in reads:
            self._wait(e, t.lw)
            if t.psum:
                for tk in list(t.rd.values()):
                    if tk[2] != e:
                        self._wait(e, tk)
        for t in writes:
            self._wait(e, t.lw)
            for tk in t.rd.values():
                self._wait(e, tk)

    def _mark(self, tk, reads, writes):
        sid = id(tk[0])
        for t in writes:
            t.lw = tk
            t.rd = {}
        for t in reads:
            t.rd[sid] = tk

    def op(self, e, fn, reads=(), writes=()):
        self.deps(e, reads, writes)
        ins = fn(self.h[e])
        self.cnt[e] += 1
        ins.then_inc(self.sem[e], 1)
        tk = (self.sem[e], self.cnt[e], e, self.epoch)
        self._mark(tk, reads, writes)
        self.nins += 1
        return tk

    def dma(self, q, ch, out, in_, reads=(), writes=(), **kw):
        self.deps(q, reads, writes)
        ins = self.h[q].dma_start(out=out, in_=in_, **kw)
        ch.cnt += 16
        ins.then_inc(ch.sem, 16)
        tk = (ch.sem, ch.cnt, None, None)
        self._mark(tk, reads, writes)
        self.nins += 1
        return tk

    def dma_multi(self, q, ch, pairs, reads=(), writes=(), **kw):
        self.deps(q, reads, writes)
        for out, in_ in pairs:
            ins = self.h[q].dma_start(out=out, in_=in_, **kw)
            ch.cnt += 16
            ins.then_inc(ch.sem, 16)
            self.nins += 1
        tk = (ch.sem, ch.cnt, None, None)
        self._mark(tk, reads, writes)
        return tk

    def newch(self, sem):
        c = DmaCh(sem)
        self.chs.append(c)
        return c

    def barrier(self):
        for e in self.h:
            for o in self.h:
                if o != e and self.cnt[o] > 0:
                    self._wait(e, (self.sem[o], self.cnt[o], o, self.epoch))
            for c in self.chs:
                if c.cnt > 0:
                    self._wait(e, (c.sem, c.cnt, None, None))


def tile_gathered(W, KCT):
    K, N = W.shape
    KH = K // (KCT * 128)
    M = N // 128
    kcr = KCT // 8
    a = W.reshape(KH, 8, kcr, 128, M, 128).transpose(1, 0, 4, 3, 2, 5)
    return np.ascontiguousarray(a).reshape(8, KH * M * 128, kcr * 128)


def tile_local(W, NW=128):
    K, N = W.shape
    KC = K // 128
    M = N // NW
    a = W.reshape(KC, 128, M, NW).transpose(2, 1, 0, 3)
    return np.ascontiguousarray(a).reshape(M * 128, KC * NW)


def fm_vec(g):
    return np.ascontiguousarray(g.reshape(-1, 128).T)


AGMAX = 512 * 1024


class Builder:
    def __init__(self, cfg, mixers="all"):
        self.c = cfg
        self.mixers = mixers
        self.nc = bass.Bass("TRN2", target_bir_lowering=False)
        self.inputs = {}
        self.uid = 0

    def din(self, name, shape, dt=F32):
        t = self.nc.dram_tensor(name, list(shape), dt, kind="ExternalInput")
        self.inputs[name] = tuple(shape)
        return t

    def dscr(self, name, shape, dt):
        self.uid += 1
        return self.nc.dram_tensor(f"{name}_{self.uid}", list(shape), dt)

    def build(self):
        c, nc = self.c, self.nc
        D, KC, TcM, TT, NT, Tc, Tq = c.D, c.KC, c.TcM, c.TT, c.NT, c.Tc, c.T
        L = c.DEPTH
        QRC, KVC, NA, HMC, HGC = c.QR // 128, c.KVR // 128, c.NA, c.HMC, c.HGC
        SQ = c.SQ
        NB = 1 + c.SEQ // 128
        KCTd = min(64, c.FF // 128)
        KHd = (c.FF // 128) // KCTd
        FH = KCTd
        do_mla = self.mixers in ("all", "mla")
        do_gdn = self.mixers in ("all", "gdn")
        mla_layers = [i for i in range(L) if i % 2 == 0] if do_mla else []
        gdn_layers = [i for i in range(L) if i % 2 == 1] if do_gdn else []
        if os.environ.get("KDBG_LAYERS"):
            act_ = [int(v) for v in os.environ["KDBG_LAYERS"].split(",")]
            mla_layers = [i for i in mla_layers if i in act_]
            gdn_layers = [i for i in gdn_layers if i in act_]
        x_in = self.din("x", [Tc, D])
        meta_in = self.din("meta", [NM, D])
        out_ext = nc.dram_tensor("out", [Tc, D], F32, kind="ExternalOutput")
        gains_in = self.din("gains", [128, L * 4 * KC])
        ident_in = self.din("ident", [128, 128])
        wup_in = [self.din(f"wup{i}", [(c.FF // 128) * 128, (KC // 8) * 128]) for i in range(L)]
        wdn_in = [self.din(f"wdn{i}", [KHd * KC * 128, (KCTd // 8) * 128]) for i in range(L)]
        hT = self.dscr("hT", [D, TcM], F32)
        hview = hT.ap().rearrange("(kc p) t -> p kc t", p=128)

        es = ExitStack()
        with es:
            def sem(name):
                return es.enter_context(nc.semaphore(name))

            engs = {"pe": nc.tensor, "act": nc.scalar, "dve": nc.vector, "pool": nc.gpsimd, "sp": nc.sync}
            sems = {n_: sem("s_" + n_) for n_ in engs}
            es.enter_context(nc.Block())
            S = Sched(nc, engs, sems)
            self.S = S
            epoch_sems = [{n_: sem(f"s{ei}_" + n_) for n_ in engs} for ei in range(L)]
            ccsem = sem("ccsem")
            self.cc_cnt = 0
            chpool = [S.newch(sem(f"ch{i}")) for i in range(56)]

            class Phase:
                def __init__(ph):
                    ph.es = ExitStack()
                    ph.chs = []

                def buf(ph, name, shape, dt):
                    self.uid += 1
                    ap = ph.es.enter_context(nc.sbuf_tensor(f"{name}_{self.uid}", list(shape), dt))
                    t = T(ap)
                    t.ch = chpool.pop()
                    ph.chs.append(t.ch)
                    return t

                def close(ph):
                    S.barrier()
                    ph.es.close()
                    chpool.extend(ph.chs)

            ps = [T(es.enter_context(nc.psum_tensor(f"ps{i}", [128, 512], F32)), psum=True) for i in range(8)]
            gph = Phase()
            ones_bf = gph.buf("ones_bf", [128, 128], BF16)
            S.op("dve", lambda h: h.memset(ones_bf.ap[:], 1.0), writes=[ones_bf])
            gains = gph.buf("gains_sb", [128, L * 4 * KC], F32)
            S.dma("sp", gains.ch, gains.ap[:], gains_in[:, :], writes=[gains])
            ident = gph.buf("ident_sb", [128, 128], F32)
            S.dma("sp", ident.ch, ident.ap[:], ident_in[:, :], writes=[ident])

            def collective(kind, groups, src, dst, src_t, dst_t):
                S.deps("pool", [src_t], [dst_t])
                op = ALU.add if kind == "ReduceScatter" else ALU.bypass
                ins = nc.gpsimd.collective_compute(kind, op, replica_groups=groups,
                                                   ins=[src.ap().opt()], outs=[dst.ap().opt()])
                self.cc_cnt += 1
                ins.then_inc(ccsem, 1)
                tk = (ccsem, self.cc_cnt, None, None)
                S._mark(tk, [src_t], [dst_t])
                S.nins += 1

            G4 = [[0, 1, 2, 3], [4, 5, 6, 7]]
            G2 = [[0, 4], [1, 5], [2, 6], [3, 7]]
            G8 = [list(range(8))]

            def allgather(src, mid, dst, t_src, t_mid, t_dst):
                collective("AllGather", G4, src, mid, t_src, t_mid)
                collective("AllGather", G2, mid, dst, t_mid, t_dst)

            cast_chs = [chpool.pop() for _ in range(4)]
            self.ncast = 0
            wt = {}

            def cast_dma(dst, src, tdst):
                ch = cast_chs[self.ncast % 4]
                self.ncast += 1
                S.dma("pool", ch, dst, src, writes=[tdst])

            def cast_and_gather(name, src):
                R, Ccols = src.shape
                tpc = max(1, (AGMAX // (Ccols * 2)) // 128)
                ntiles = R // 128
                chunks = []
                for c0 in range(0, ntiles, tpc):
                    nt_ = min(tpc, ntiles - c0)
                    rows = nt_ * 128
                    b_ = self.dscr(f"{name}_b", [rows, Ccols], BF16)
                    m_ = self.dscr(f"{name}_m", [4 * rows, Ccols], BF16)
                    g_ = self.dscr(f"{name}_g", [8 * rows, Ccols], BF16)
                    tb, tm, tg = T(b_), T(m_), T(g_)
                    cast_dma(b_[:, :], src[c0 * 128:c0 * 128 + rows, :], tb)
                    allgather(b_, m_, g_, tb, tm, tg)
                    chunks.append((g_, tg, nt_))
                wt[name] = (chunks, tpc)

            lw = {}

            def local_w(name, shape):
                R, Cc = shape
                if Cc > 2048:
                    a_ = -(-Cc // 2048)
                    R, Cc = R * a_, Cc // a_
                src = self.din(name, [R, Cc])
                dst = self.dscr(name + "_lb", [R, Cc], BF16)
                td = T(dst)
                assert Cc <= 2048
                step = 1024
                for r0 in range(0, R, step):
                    r1 = min(R, r0 + step)
                    cast_dma(dst[r0:r1, :], src[r0:r1, :], td)
                lw[name] = (dst, td)

            for i in mla_layers:
                j = i // 2
                wa_in = self.din(f"wa{j}", [NA * 128, (KC // 8) * 128])
                cast_and_gather(f"wa{j}", wa_in)
                local_w(f"wqb{j}", [HMC * 2 * 128, QRC * 128])
                local_w(f"wkk{j}", [HMC * 128, KVC * 128])
                cwv = min(1024, KVC * HMC * 128)
                local_w(f"wkv{j}", [128 * KVC * HMC * 128 // cwv, cwv])
                local_w(f"wol{j}", [KC * 128, HMC * 128])
                if i == 0:
                    cast_and_gather("wup0", wup_in[0])
                    cast_and_gather("wdn0", wdn_in[0])
            for i in gdn_layers:
                j = i // 2
                local_w(f"wgq{j}", [HGC * 4 * 128, KC * 128])
                local_w(f"wgba{j}", [128, KC * 2 * HGC])
                local_w(f"wgo{j}", [KC * 128, HGC * 128])
            for i in range(L):
                if f"wup{i}" not in wt:
                    cast_and_gather(f"wup{i}", wup_in[i])
                    cast_and_gather(f"wdn{i}", wdn_in[i])
            if mla_layers:
                qkg_in = self.din("qkg", [128, ((L + 1) // 2) * (NA - 1)])
                qkg = gph.buf("qkg_sb", [128, ((L + 1) // 2) * (NA - 1)], F32)
                S.dma("sp", qkg.ch, qkg.ap[:], qkg_in[:, :], writes=[qkg])
                cos_in = self.din("ropecos", [64, Tq])
                sin_in = self.din("ropesin", [64, Tq])
                negms = [gph.buf("negm", [128, HMC], F32) for _ in range((L + 1) // 2)]
                tri_in = self.din("tri", [128, 128])
                tri = gph.buf("tri_sb", [128, 128], BF16)
                trif = gph.buf("trif_sb", [128, 128], F32)
                S.dma("sp", trif.ch, trif.ap[:], tri_in[:, :], writes=[trif])
                S.op("dve", lambda h: h.tensor_copy(out=tri.ap[:], in_=trif.ap[:]), reads=[trif], writes=[tri])

            if gdn_layers:
                ng = L // 2
                gconst_in = self.din("gconst", [128, 4 * 128])
                gconst = gph.buf("gconst_sb", [128, 4 * 128], F32)
                S.dma("sp", gconst.ch, gconst.ap[:], gconst_in[:, :], writes=[gconst])
                gmask_in = self.din("gmask", [128, 14 * 128])
                gmask = gph.buf("gmask_sb", [128, 14 * 128], F32)
                S.dma("sp", gmask.ch, gmask.ap[:], gmask_in[:, :], writes=[gmask])
                NGP = HGC * 12 + 2 * HGC + 128
                gpar_in = self.din("gpar", [128, ng * NGP])
                gpar = gph.buf("gpar_sb", [128, ng * NGP], F32)
                S.dma("sp", gpar.ch, gpar.ap[:], gpar_in[:, :], writes=[gpar])
                nexpA = gph.buf("nexpA", [128, ng * HGC], F32)
                for jj in range(ng):
                    o_ = jj * NGP + HGC * 12
                    S.op("act", lambda h: h.activation(out=nexpA.ap[:, jj * HGC:(jj + 1) * HGC], in_=gpar.ap[:, o_:o_ + HGC], func=AF.Exp), reads=[gpar], writes=[nexpA])
                S.op("dve", lambda h: h.tensor_scalar(out=nexpA.ap[:, :], in0=nexpA.ap[:, :], scalar1=-1.0, scalar2=None, op0=ALU.mult), reads=[nexpA], writes=[nexpA])

            t_hT = T(hT)
            ph = Phase()
            xin = [ph.buf("xin", [128, D], F32) for i in range(2)]
            xtr = [ph.buf("xtr", [128, KC, 128], F32) for i in range(2)]
            blocks = [("meta", 0, NM)] + [("x", b * 128, 128) for b in range(Tc // 128)]
            for bi, (src, r0, nr) in enumerate(blocks):
                xb, xt = xin[bi % 2], xtr[bi % 2]
                srcap = meta_in[0:NM, :] if src == "meta" else x_in[r0:r0 + nr, :]
                S.dma("sp", xb.ch, xb.ap[0:nr, :], srcap, writes=[xb])
                for kc in range(KC):
                    p = ps[kc % 4]
                    S.op("pe", lambda h: h.transpose(p.ap[:, 0:nr], xb.ap[0:nr, kc * 128:(kc + 1) * 128], ident.ap[0:nr, 0:nr]),
                         reads=[xb, ident], writes=[p])
                    if kc % 2 == 0:
                        S.op("act", lambda h: h.activation(out=xt.ap[:, kc, 0:nr], in_=p.ap[:, 0:nr], func=AF.Copy),
                             reads=[p], writes=[xt])
                    else:
                        S.op("dve", lambda h: h.tensor_copy(out=xt.ap[:, kc, 0:nr], in_=p.ap[:, 0:nr]),
                             reads=[p], writes=[xt])
                col0 = 0 if src == "meta" else NM + r0
                S.dma("pool", xt.ch, hview[:, :, col0:col0 + nr], xt.ap[:, :, 0:nr], reads=[xt], writes=[t_hT])
            ph.close()

            n = TT
            self.wctr = 0

            def gcol(layer, which, kc):
                o = (layer * 4 + which) * KC + kc
                return gains.ap[:, o:o + 1]

            def tp_bufs(ph, ncols, wcols=64 * 128):
                B = {}
                B["sq"] = [ph.buf("sq", [128, ncols], BF16) for _ in range(2)]
                B["rstd"] = ph.buf("rstd", [128, ncols], F32)
                B["rtmp"] = [ph.buf("rtmp", [128, ncols], F32) for _ in range(2)]
                B["wbuf"] = [ph.buf("wbuf", [128, wcols], BF16) for _ in range(4)]
                return B

            def load_wtile(B, name, tile, kcr):
                b = self.wctr % 4
                self.wctr += 1
                wb = B["wbuf"][b]
                chunks, tpc = wt[name]
                g_, tg, nt_ = chunks[tile // tpc]
                src = g_.ap().rearrange("(r t p) f -> t p r f", r=8, t=nt_, p=128)[tile % tpc]
                dst = wb.ap[:, 0:8 * kcr * 128].rearrange("p (r f) -> p r f", r=8)
                S.dma("sp", wb.ch, dst, src, reads=[tg], writes=[wb])
                return wb

            def load_ltile(B, name, row0, width):
                b = self.wctr % 4
                self.wctr += 1
                wb = B["wbuf"][b]
                dst_, td = lw[name]
                a_ = width // dst_.shape[1]
                src = dst_.ap().rearrange("(t p a) f -> t p a f", p=128, a=a_)[row0 // 128]
                S.dma("sp", wb.ch, wb.ap[:, 0:width].rearrange("p (a f) -> p a f", a=a_), src, reads=[td], writes=[wb])
                return wb

            def norm_stats(B, chunks, reads, Dn, ncols, rows=128):
                acc = ps[7]
                nch = len(chunks)
                for i, ap in enumerate(chunks):
                    s = B["sq"][i % 2]
                    S.op("act", lambda h: h.activation(out=s.ap[0:rows, 0:ncols], in_=ap, func=AF.Square), reads=reads, writes=[s])
                    S.op("pe", lambda h: h.matmul(acc.ap[:, 0:ncols], ones_bf.ap[0:rows, :], s.ap[0:rows, 0:ncols], start=(i == 0), stop=(i == nch - 1)),
                         reads=[s, ones_bf], writes=[acc])
                rstd = B["rstd"]
                S.op("act", lambda h: h.activation(out=rstd.ap[:, 0:ncols], in_=acc.ap[:, 0:ncols], func=AF.Sqrt, scale=1.0 / Dn, bias=EPS),
                     reads=[acc], writes=[rstd])
                S.op("dve", lambda h: h.reciprocal(out=rstd.ap[:, 0:ncols], in_=rstd.ap[:, 0:ncols]), reads=[rstd], writes=[rstd])

            def mlp_phase(layer, mix=None):
                ph = Phase()
                B = tp_bufs(ph, n)
                hbuf = ph.buf("hbuf", [128, KC, n], F32)
                mbuf = ph.buf("mbuf", [128, KC, n], F32)
                hn = ph.buf("hn", [128, KC, n], BF16)
                ffb = ph.buf("ffb", [128, FH, n], BF16)
                rstd, rtmp = B["rstd"], B["rtmp"]
                for ti in range(NT):
                    c0 = ti * n
                    S.dma("pool", hbuf.ch, hbuf.ap[:, :, :], hview[:, :, c0:c0 + n], reads=[t_hT], writes=[hbuf])
                    if mix is not None:
                        pairs = []
                        rds = []
                        for k2, (mt, tmt) in enumerate(mix):
                            kc, half = k2 // 2, k2 % 2
                            pairs.append((mbuf.ap[half * 64:(half + 1) * 64, kc, :], mt[:, c0:c0 + n]))
                            rds.append(tmt)
                        S.dma_multi("pool", mbuf.ch, pairs, reads=rds, writes=[mbuf])
                        norm_stats(B, [mbuf.ap[:, kc, :] for kc in range(KC)], [mbuf], D, n)
                        for kc in range(KC):
                            r = rtmp[kc % 2]
                            S.op("dve", lambda h: h.scalar_tensor_tensor(out=r.ap[:, :], in0=mbuf.ap[:, kc, :], scalar=gcol(layer, 1, kc),
                                                                           in1=rstd.ap[:, :], op0=ALU.mult, op1=ALU.mult),
                                 reads=[mbuf, rstd, gains], writes=[r])
                            S.op("pool", lambda h: h.tensor_tensor(out=hbuf.ap[:, kc, :], in0=hbuf.ap[:, kc, :], in1=r.ap[:, :], op=ALU.add),
                                 reads=[r, hbuf], writes=[hbuf])
                    norm_stats(B, [hbuf.ap[:, kc, :] for kc in range(KC)], [hbuf], D, n)
                    for kc in range(KC):
                        S.op("dve", lambda h: h.scalar_tensor_tensor(out=hn.ap[:, kc, :], in0=hbuf.ap[:, kc, :], scalar=gcol(layer, 2, kc),
                                                                       in1=rstd.ap[:, :], op0=ALU.mult, op1=ALU.mult),
                             reads=[hbuf, rstd, gains], writes=[hn])
                    pi = 0
                    for kh in range(KHd):
                        for f in range(FH):
                            fc = kh * FH + f
                            wb = load_wtile(B, f"wup{layer}", fc, KC // 8)
                            p = ps[pi % 6]
                            pi += 1
                            for kc in range(KC):
                                S.op("pe", lambda h: h.matmul(p.ap[:, 0:n], wb.ap[:, kc * 128:(kc + 1) * 128], hn.ap[:, kc, :],
                                                              start=(kc == 0), stop=(kc == KC - 1)),
                                     reads=[wb, hn], writes=[p])
                            r = rtmp[fc % 2]
                            S.op("act", lambda h: h.activation(out=r.ap[:, :], in_=p.ap[:, 0:n], func=AF.Relu), reads=[p], writes=[r])
                            S.op("dve", lambda h: h.tensor_tensor(out=ffb.ap[:, f, :], in0=r.ap[:, :], in1=r.ap[:, :], op=ALU.mult),
                                 reads=[r], writes=[ffb])
                        for m in range(KC):
                            wb = load_wtile(B, f"wdn{layer}", kh * KC + m, FH // 8)
                            p = ps[pi % 6]
                            pi += 1
                            for f in range(FH):
                                S.op("pe", lambda h: h.matmul(p.ap[:, 0:n], wb.ap[:, f * 128:(f + 1) * 128], ffb.ap[:, f, :],
                                                              start=(f == 0), stop=(f == FH - 1)),
                                     reads=[wb, ffb], writes=[p])
                            if kh == 0:
                                S.op("act", lambda h: h.activation(out=mbuf.ap[:, m, :], in_=p.ap[:, 0:n], func=AF.Copy),
                                     reads=[p], writes=[mbuf])
                            else:
                                S.op("dve", lambda h: h.tensor_tensor(out=mbuf.ap[:, m, :], in0=mbuf.ap[:, m, :], in1=p.ap[:, 0:n], op=ALU.add),
                                     reads=[p, mbuf], writes=[mbuf])
                    norm_stats(B, [mbuf.ap[:, kc, :] for kc in range(KC)], [mbuf], D, n)
                    for kc in range(KC):
                        r = rtmp[kc % 2]
                        S.op("dve", lambda h: h.scalar_tensor_tensor(out=r.ap[:, :], in0=mbuf.ap[:, kc, :], scalar=gcol(layer, 3, kc),
                                                                       in1=rstd.ap[:, :], op0=ALU.mult, op1=ALU.mult),
                             reads=[mbuf, rstd, gains], writes=[r])
                        S.op("pool", lambda h: h.tensor_tensor(out=hbuf.ap[:, kc, :], in0=hbuf.ap[:, kc, :], in1=r.ap[:, :], op=ALU.add),
                             reads=[r, hbuf], writes=[hbuf])
                    S.dma("pool", hbuf.ch, hview[:, :, c0:c0 + n], hbuf.ap[:, :, :], reads=[hbuf], writes=[t_hT])
                ph.close()

            seq_tiles = [(0, 0, 0, NM)]
            for i in range(c.SEQ // SQ):
                r = (i * SQ) // Tc
                seq_tiles.append((r, NM + (i * SQ) % Tc, NM + i * SQ, SQ))

            def gather_rows(local, t_local, R, rc, name):
                out = []
                for r0 in range(0, R, rc):
                    b_ = local[r0 // rc]
                    m_ = self.dscr(name + "_m", [4 * rc, TcM], BF16)
                    g_ = self.dscr(name + "_g", [8 * rc, TcM], BF16)
                    tm, tg = T(m_), T(g_)
                    allgather(b_, m_, g_, t_local[r0 // rc], tm, tg)
                    out.append((g_, tg))
                return out

            def rs_parts(name):
                parts = []
                for k2 in range(D // 64):
                    p_ = self.dscr(name + "_p", [8 * 64, TcM], F32)
                    o_ = self.dscr(name + "_o", [64, TcM], F32)
                    parts.append((p_, T(p_), o_, T(o_)))
                return parts

            def wo_pass(ph, B, wname, nh, OT, parts):
                otb = [ph.buf("otb", [128, nh, SQ], BF16) for _ in range(2)]
                pst = [ph.buf("pst", [128, SQ], F32) for _ in range(4)]
                pi = 0
                for si, (r, c0, s0, nq) in enumerate(seq_tiles):
                    ob = otb[si % 2]
                    S.dma("pool", ob.ch, ob.ap[:, :, 0:nq], OT.ap()[:, :, s0:s0 + nq].rearrange("h p t -> p h t"), writes=[ob])
                    for m in range(KC):
                        wb = load_ltile(B, wname, m * 128, nh * 128)
                        p = ps[pi % 6]
                        st = pst[pi % 4]
                        pi += 1
                        for hh in range(nh):
                            S.op("pe", lambda h: h.matmul(p.ap[:, 0:nq], wb.ap[:, hh * 128:(hh + 1) * 128], ob.ap[:, hh, 0:nq],
                                                          start=(hh == 0), stop=(hh == nh - 1)), reads=[wb, ob], writes=[p])
                        if pi % 2 == 0:
                            S.op("act", lambda h: h.activation(out=st.ap[:, 0:nq], in_=p.ap[:, 0:nq], func=AF.Copy), reads=[p], writes=[st])
                        else:
                            S.op("dve", lambda h: h.tensor_copy(out=st.ap[:, 0:nq], in_=p.ap[:, 0:nq]), reads=[p], writes=[st])
                        pairs = []
                        ranks = range(8) if si == 0 else [r]
                        for half in range(2):
                            p_ = parts[2 * m + half][0]
                            for rr in ranks:
                                pairs.append((p_[rr * 64:(rr + 1) * 64, c0:c0 + nq], st.ap[half * 64:(half + 1) * 64, 0:nq]))
                        S.dma_multi("pool", st.ch, pairs, reads=[st])
                return

            def reduce_scatter(parts):
                S.barrier()
                mix = []
                for (p_, tp, o_, to) in parts:
                    m_ = self.dscr("rsmid", [4 * 64, TcM], F32)
                    tm = T(m_)
                    collective("ReduceScatter", G2, p_, m_, tp, tm)
                    collective("ReduceScatter", G4, m_, o_, tm, to)
                    mix.append((o_, to))
                return mix

            scale = (128 + 64) ** -0.5

            def mla_layer(layer):
                j = layer // 2
                rc = 64
                lat_l = [self.dscr("latl", [rc, TcM], BF16) for _ in range(NA * 2)]
                t_lat = [T(x_) for x_ in lat_l]
                ph = Phase()
                B = tp_bufs(ph, n)
                hbuf = ph.buf("hbuf", [128, KC, n], F32)
                hn = ph.buf("hn", [128, KC, n], BF16)
                cqf = ph.buf("cqf", [128, NA, n], F32)
                latb = ph.buf("latb", [128, NA, n], BF16)
                rstd = B["rstd"]
                for ti in range(NT):
                    c0 = ti * n
                    S.dma("pool", hbuf.ch, hbuf.ap[:, :, :], hview[:, :, c0:c0 + n], reads=[t_hT], writes=[hbuf])
                    norm_stats(B, [hbuf.ap[:, kc, :] for kc in range(KC)], [hbuf], D, n)
                    for kc in range(KC):
                        S.op("dve", lambda h: h.scalar_tensor_tensor(out=hn.ap[:, kc, :], in0=hbuf.ap[:, kc, :], scalar=gcol(layer, 0, kc),
                                                                       in1=rstd.ap[:, :], op0=ALU.mult, op1=ALU.mult),
                             reads=[hbuf, rstd, gains], writes=[hn])
                    for m in range(NA):
                        wb = load_wtile(B, f"wa{j}", m, KC // 8)
                        p = ps[m % 6]
                        for kc in range(KC):
                            S.op("pe", lambda h: h.matmul(p.ap[:, 0:n], wb.ap[:, kc * 128:(kc + 1) * 128], hn.ap[:, kc, :],
                                                          start=(kc == 0), stop=(kc == KC - 1)), reads=[wb, hn], writes=[p])
                        S.op("act", lambda h: h.activation(out=cqf.ap[:, m, :], in_=p.ap[:, 0:n], func=AF.Copy), reads=[p], writes=[cqf])
                    for (m0, m1, Dn) in ((0, QRC, c.QR), (QRC, QRC + KVC, c.KVR)):
                        norm_stats(B, [cqf.ap[:, m, :] for m in range(m0, m1)], [cqf], Dn, n)
                        for m in range(m0, m1):
                            go = j * (NA - 1) + m
                            S.op("dve", lambda h: h.scalar_tensor_tensor(out=latb.ap[:, m, :], in0=cqf.ap[:, m, :], scalar=qkg.ap[:, go:go + 1],
                                                                           in1=rstd.ap[:, :], op0=ALU.mult, op1=ALU.mult),
                                 reads=[cqf, rstd, qkg], writes=[latb])
                    S.op("dve", lambda h: h.tensor_copy(out=latb.ap[:, NA - 1, :], in_=cqf.ap[:, NA - 1, :]), reads=[cqf], writes=[latb])
                    pairs = []
                    for m in range(NA):
                        for half in range(2):
                            pairs.append((lat_l[2 * m + half][:, c0:c0 + n], latb.ap[half * 64:(half + 1) * 64, m, :]))
                    S.dma_multi("pool", latb.ch, pairs, reads=[latb], writes=t_lat)
                ph.close()
                import os
                STOP = os.environ.get("KDBG_STOP", "")
                if STOP == "a1":
                    return None
                latg = gather_rows(lat_l, t_lat, NA * 128, rc, "lat")
                if STOP == "ag":
                    S.barrier()
                    return None

                def lat_rows(m, half, r, c0, nq):
                    g_, tg = latg[2 * m + half]
                    return g_.ap().rearrange("(r x) t -> x r t", r=8)[:, r, c0:c0 + nq], tg

                QN = self.dscr("QN", [HMC, 128, Tq], BF16)
                QP = self.dscr("QP", [HMC, 64, Tq], BF16)
                KN = self.dscr("KN", [HMC, 128, Tq], BF16)
                KP = self.dscr("KP", [64, Tq], BF16)
                Vs = self.dscr("Vs", [HMC, 128, NB, 128], BF16)
                OT = self.dscr("OT", [HMC, 128, Tq], BF16)
                ph = Phase()
                B = tp_bufs(ph, SQ)
                latq = [ph.buf("latq", [128, NA - 1, SQ], BF16) for _ in range(2)]
                pe_r = [ph.buf("pe_r", [64, 2, SQ], BF16) for _ in range(2)]
                cs = [ph.buf("cs", [64, 2, SQ], F32) for _ in range(2)]
                wv = ph.buf("wv", [128, KVC * HMC * 128], BF16)
                dst_, td = lw[f"wkv{j}"]
                S.dma("sp", wv.ch, wv.ap[:, :], dst_.ap().rearrange("(p a) f -> p (a f)", p=128), reads=[td], writes=[wv])
                kpe2 = ph.buf("kpe2", [128, SQ], F32)
                t1 = [ph.buf("t1", [64, SQ], F32) for _ in range(2)]
                t2 = [ph.buf("t2", [64, SQ], F32) for _ in range(2)]
                stg = [ph.buf("stg", [128, SQ], BF16) for _ in range(4)]
                stp = [ph.buf("stp", [64, SQ], BF16) for _ in range(3)]
                vst = [ph.buf("vst", [128, HMC * 128], BF16) for _ in range(2)]
                tot = ph.buf("tot", [128, SQ], F32)
                mx = ph.buf("mx", [128, 4], F32)
                qmax2 = ph.buf("qmax2", [128, HMC], F32)
                kmax2 = ph.buf("kmax2", [128, HMC], F32)
                S.op("dve", lambda h: h.memset(qmax2.ap[:], 0.0), writes=[qmax2])
                S.op("dve", lambda h: h.memset(kmax2.ap[:], 0.0), writes=[kmax2])
                sg = 0
                blk_ctr = 0
                for si, (r, c0, s0, nq) in enumerate(seq_tiles):
                    lq, pr, csb = latq[si % 2], pe_r[si % 2], cs[si % 2]
                    pairs, rds = [], []
                    for m in range(NA - 1):
                        for half in range(2):
                            ap_, tg = lat_rows(m, half, r, c0, nq)
                            pairs.append((lq.ap[half * 64:(half + 1) * 64, m, 0:nq], ap_))
                            rds.append(tg)
                    S.dma_multi("pool", lq.ch, pairs, reads=rds, writes=[lq])
                    pairs, rds = [], []
                    for half in range(2):
                        ap_, tg = lat_rows(NA - 1, half, r, c0, nq)
                        pairs.append((pr.ap[:, half, 0:nq], ap_))
                        rds.append(tg)
                    S.dma_multi("pool", pr.ch, pairs, reads=rds, writes=[pr])
                    S.dma_multi("sp", csb.ch, [(csb.ap[:, 0, 0:nq], cos_in[:, s0:s0 + nq]), (csb.ap[:, 1, 0:nq], sin_in[:, s0:s0 + nq])], writes=[csb])

                    def rope(src0, src1, rd, dst):
                        a, b_ = t1[sg % 2], t2[sg % 2]
                        S.op("dve", lambda h: h.tensor_tensor(out=a.ap[:, 0:nq], in0=src0, in1=csb.ap[:, 0, 0:nq], op=ALU.mult), reads=rd + [csb], writes=[a])
                        S.op("dve", lambda h: h.tensor_tensor(out=b_.ap[:, 0:nq], in0=src1, in1=csb.ap[:, 1, 0:nq], op=ALU.mult), reads=rd + [csb], writes=[b_])
                        S.op("pool", lambda h: h.tensor_tensor(out=dst.ap[:, 0:nq], in0=a.ap[:, 0:nq], in1=b_.ap[:, 0:nq], op=ALU.add), reads=[a, b_], writes=[dst])

                    def norm2(pieces, rd, extra, acc_t, hh):
                        pn = ps[6]
                        for i_, (ap_, rows) in enumerate(pieces):
                            s = B["sq"][i_ % 2]
                            S.op("act", lambda h: h.activation(out=s.ap[0:rows, 0:nq], in_=ap_, func=AF.Square), reads=rd, writes=[s])
                            S.op("pe", lambda h: h.matmul(pn.ap[:, 0:nq], ones_bf.ap[0:rows, :], s.ap[0:rows, 0:nq], start=(i_ == 0), stop=(i_ == len(pieces) - 1)),
                                 reads=[s, ones_bf], writes=[pn])
                        if extra is not None:
                            S.op("dve", lambda h: h.tensor_tensor(out=tot.ap[:, 0:nq], in0=pn.ap[:, 0:nq], in1=extra.ap[:, 0:nq], op=ALU.add), reads=[pn, extra], writes=[tot])
                            S.op("dve", lambda h: h.reduce_max(out=mx.ap[:, 0:1], in_=tot.ap[:, 0:nq], axis=AX.X), reads=[tot], writes=[mx])
                        else:
                            S.op("dve", lambda h: h.reduce_max(out=mx.ap[:, 0:1], in_=pn.ap[:, 0:nq], axis=AX.X), reads=[pn], writes=[mx])
                        S.op("dve", lambda h: h.tensor_tensor(out=acc_t.ap[:, hh:hh + 1], in0=acc_t.ap[:, hh:hh + 1], in1=mx.ap[:, 0:1], op=ALU.max), reads=[mx, acc_t], writes=[acc_t])

                    kp = stp[2]
                    rope(pr.ap[:, 0, 0:nq], pr.ap[:, 1, 0:nq], [pr], kp)
                    sg += 1
                    S.dma("pool", kp.ch, KP[:, s0:s0 + nq], kp.ap[:, 0:nq], reads=[kp])
                    pk = ps[6]
                    s_ = B["sq"][0]
                    S.op("act", lambda h: h.activation(out=s_.ap[0:64, 0:nq], in_=kp.ap[:, 0:nq], func=AF.Square), reads=[kp], writes=[s_])
                    S.op("pe", lambda h: h.matmul(pk.ap[:, 0:nq], ones_bf.ap[0:64, :], s_.ap[0:64, 0:nq], start=True, stop=True), reads=[s_, ones_bf], writes=[pk])
                    S.op("act", lambda h: h.activation(out=kpe2.ap[:, 0:nq], in_=pk.ap[:, 0:nq], func=AF.Copy), reads=[pk], writes=[kpe2])
                    for hh in range(HMC):
                        wb = load_ltile(B, f"wqb{j}", (2 * hh) * 128, QRC * 128)
                        pq = ps[0]
                        for kc in range(QRC):
                            S.op("pe", lambda h: h.matmul(pq.ap[:, 0:nq], wb.ap[:, kc * 128:(kc + 1) * 128], lq.ap[:, kc, 0:nq], start=(kc == 0), stop=(kc == QRC - 1)),
                                 reads=[wb, lq], writes=[pq])
                        st = stg[(2 * hh) % 4]
                        S.op("act", lambda h: h.activation(out=st.ap[:, 0:nq], in_=pq.ap[:, 0:nq], func=AF.Copy), reads=[pq], writes=[st])
                        S.dma("pool", st.ch, QN[hh, :, s0:s0 + nq], st.ap[:, 0:nq], reads=[st])
                        wb2 = load_ltile(B, f"wqb{j}", (2 * hh + 1) * 128, QRC * 128)
                        pa, pb = ps[1], ps[2]
                        for kc in range(QRC):
                            S.op("pe", lambda h: h.matmul(pa.ap[0:64, 0:nq], wb2.ap[:, kc * 128:kc * 128 + 64], lq.ap[:, kc, 0:nq], start=(kc == 0), stop=(kc == QRC - 1)),
                                 reads=[wb2, lq], writes=[pa])
                        for kc in range(QRC):
                            S.op("pe", lambda h: h.matmul(pb.ap[0:64, 0:nq], wb2.ap[:, kc * 128 + 64:(kc + 1) * 128], lq.ap[:, kc, 0:nq], start=(kc == 0), stop=(kc == QRC - 1)),
                                 reads=[wb2, lq], writes=[pb])
                        qp = stp[hh % 2]
                        rope(pa.ap[0:64, 0:nq], pb.ap[0:64, 0:nq], [pa, pb], qp)
                        sg += 1
                        S.dma("pool", qp.ch, QP[hh, :, s0:s0 + nq], qp.ap[:, 0:nq], reads=[qp])
                        norm2([(st.ap[:, 0:nq], 128), (qp.ap[:, 0:nq], 64)], [st, qp], None, qmax2, hh)
                        wb3 = load_ltile(B, f"wkk{j}", hh * 128, KVC * 128)
                        pkn = ps[3]
                        for kc in range(KVC):
                            S.op("pe", lambda h: h.matmul(pkn.ap[:, 0:nq], wb3.ap[:, kc * 128:(kc + 1) * 128], lq.ap[:, QRC + kc, 0:nq], start=(kc == 0), stop=(kc == KVC - 1)),
                                 reads=[wb3, lq], writes=[pkn])
                        st2 = stg[(2 * hh + 1) % 4]
                        S.op("act", lambda h: h.activation(out=st2.ap[:, 0:nq], in_=pkn.ap[:, 0:nq], func=AF.Copy), reads=[pkn], writes=[st2])
                        S.dma("pool", st2.ch, KN[hh, :, s0:s0 + nq], st2.ap[:, 0:nq], reads=[st2])
                        norm2([(st2.ap[:, 0:nq], 128)], [st2], kpe2, kmax2, hh)
                    nblk = 1 if si == 0 else SQ // 128
                    for bq in range(nblk):
                        nk = NM if si == 0 else 128
                        blk = 0 if si == 0 else 1 + (s0 - NM) // 128 + bq
                        vb = vst[blk_ctr % 2]
                        blk_ctr += 1
                        W_ = HMC * 128
                        for hf in range(0, W_, 512):
                            wd = min(512, W_ - hf)
                            pv = ps[4 + (hf // 512) % 2]
                            for kc in range(KVC):
                                S.op("pe", lambda h: h.matmul(pv.ap[0:nk, 0:wd], lq.ap[:, QRC + kc, bq * 128:bq * 128 + nk], wv.ap[:, kc * W_ + hf:kc * W_ + hf + wd],
                                                              start=(kc == 0), stop=(kc == KVC - 1)), reads=[lq, wv], writes=[pv])
                            S.op("act", lambda h: h.activation(out=vb.ap[0:nk, hf:hf + wd], in_=pv.ap[0:nk, 0:wd], func=AF.Copy), reads=[pv], writes=[vb])
                        S.dma("pool", vb.ch, Vs.ap()[:, 0:nk, blk, :].rearrange("h p d -> p h d"),
                              vb.ap[0:nk, :].rearrange("p (h d) -> p h d", h=HMC), reads=[vb])
                negm = negms[j]
                S.op("dve", lambda h: h.tensor_tensor(out=negm.ap[:, :], in0=qmax2.ap[:, :], in1=kmax2.ap[:, :], op=ALU.mult), reads=[qmax2, kmax2], writes=[negm])
                S.op("act", lambda h: h.activation(out=negm.ap[:, :], in_=negm.ap[:, :], func=AF.Sqrt), reads=[negm], writes=[negm])
                S.op("dve", lambda h: h.tensor_scalar(out=negm.ap[:, :], in0=negm.ap[:, :], scalar1=-scale, scalar2=None, op0=ALU.mult), reads=[negm], writes=[negm])
                ph.close()
                if STOP == "a2a":
                    return None

                ph = Phase()
                kpT = ph.buf("kpT", [64, Tq], BF16)
                S.dma("sp", kpT.ch, kpT.ap[:, :], KP[:, :], writes=[kpT])
                knT = ph.buf("knT", [128, Tq], BF16)
                vT = ph.buf("vT", [128, NB, 128], BF16)
                qn = [ph.buf("qn", [128, SQ], BF16) for _ in range(2)]
                qp = [ph.buf("qp", [64, SQ], BF16) for _ in range(2)]
                pT = [ph.buf("pT", [128, SQ], BF16) for _ in range(3)]
                rl = ph.buf("rl", [128, SQ], F32)
                ob = [ph.buf("ob", [128, SQ], BF16) for _ in range(2)]
                ui = 0
                ci = 0
                SQB = SQ // 128
                for hh in range(HMC):
                    S.dma("sp", knT.ch, knT.ap[:, :], KN[hh, :, :], writes=[knT])
                    S.dma("sp", vT.ch, vT.ap[:, :, :], Vs[hh, :, :, :], writes=[vT])
                    for si, (r, c0, s0, nq) in enumerate(seq_tiles):
                        qa, qb = qn[ci % 2], qp[ci % 2]
                        S.dma("pool", qa.ch, qa.ap[:, 0:nq], QN[hh, :, s0:s0 + nq], writes=[qa])
                        S.dma("pool", qb.ch, qb.ap[:, 0:nq], QP[hh, :, s0:s0 + nq], writes=[qb])
                        o_ps, l_ps = ps[3 + ci % 2], ps[5 + ci % 2]
                        if si == 0:
                            kbs = [(0, NM, 0, 0, NM)]
                        else:
                            jq = (s0 - NM) // 128
                            kbs = [(0, NM, 0, 0, 0)] + [(b, 128, NM + (b - 1) * 128, 0, 0) for b in range(1, jq + 1)]
                            kbs += [(jq + 1 + d, 128, NM + (jq + d) * 128, 128 * d, 128) for d in range(SQB)]
                        for ki, (blk, nk, k0, clo, dw) in enumerate(kbs):
                            sp_ = ps[ui % 3]
                            pt = pT[ui % 3]
                            ui += 1
                            S.op("pe", lambda h: h.matmul(sp_.ap[0:nk, clo:nq], knT.ap[:, k0:k0 + nk], qa.ap[:, clo:nq], start=True, stop=False),
                                 reads=[knT, qa], writes=[sp_])
                            S.op("pe", lambda h: h.matmul(sp_.ap[0:nk, clo:nq], kpT.ap[:, k0:k0 + nk], qb.ap[:, clo:nq], start=False, stop=True),
                                 reads=[kpT, qb], writes=[sp_])
                            S.op("act", lambda h: h.activation(out=pt.ap[0:nk, clo:nq], in_=sp_.ap[0:nk, clo:nq], func=AF.Exp, scale=scale,
                                                               bias=negm.ap[0:nk, hh:hh + 1]), reads=[sp_, negm], writes=[pt])
                            if dw:
                                S.op("pool", lambda h: h.tensor_tensor(out=pt.ap[0:nk, clo:clo + dw], in0=pt.ap[0:nk, clo:clo + dw], in1=tri.ap[0:nk, 0:dw], op=ALU.mult),
                                     reads=[pt, tri], writes=[pt])
                            first, last = (ki == 0), (ki == len(kbs) - 1)
                            S.op("pe", lambda h: h.matmul(o_ps.ap[:, clo:nq], vT.ap[0:nk, blk, :], pt.ap[0:nk, clo:nq], start=first, stop=last),
                                 reads=[vT, pt], writes=[o_ps])
                            S.op("pe", lambda h: h.matmul(l_ps.ap[:, clo:nq], ones_bf.ap[0:nk, :], pt.ap[0:nk, clo:nq], start=first, stop=last),
                                 reads=[ones_bf, pt], writes=[l_ps])
                        S.op("dve", lambda h: h.reciprocal(out=rl.ap[:, 0:nq], in_=l_ps.ap[:, 0:nq]), reads=[l_ps], writes=[rl])
                        o_ = ob[ci % 2]
                        S.op("dve", lambda h: h.tensor_tensor(out=o_.ap[:, 0:nq], in0=o_ps.ap[:, 0:nq], in1=rl.ap[:, 0:nq], op=ALU.mult), reads=[o_ps, rl], writes=[o_])
                        S.dma("pool", o_.ch, OT[hh, :, s0:s0 + nq], o_.ap[:, 0:nq], reads=[o_])
                        ci += 1
                ph.close()
                if STOP == "attn":
                    return None

                parts = rs_parts("mla")
                ph = Phase()
                B = tp_bufs(ph, 8)
                wo_pass(ph, B, f"wol{j}", HMC, OT, parts)
                ph.close()
                if STOP == "wo":
                    return None
                return reduce_scatter(parts)

            def gdn_layer(layer):
                j = layer // 2
                NGP = HGC * 12 + 2 * HGC + 128
                gp0 = j * NGP
                UT = gconst.ap[:, 0:128]
                SUT = gconst.ap[:, 128:256]
                SLT = gconst.ap[:, 256:384]
                ONESF = gconst.ap[:, 384:512]
                rc = 64
                hn_l = [self.dscr("hnl", [rc, TcM], BF16) for _ in range(D // rc)]
                t_hn = [T(x_) for x_ in hn_l]
                ph = Phase()
                B = tp_bufs(ph, n, wcols=128)
                hbuf = ph.buf("hbuf", [128, KC, n], F32)
                hn = ph.buf("hn", [128, KC, n], BF16)
                rstd = B["rstd"]
                for ti in range(NT):
                    c0 = ti * n
                    S.dma("pool", hbuf.ch, hbuf.ap[:, :, :], hview[:, :, c0:c0 + n], reads=[t_hT], writes=[hbuf])
                    norm_stats(B, [hbuf.ap[:, kc, :] for kc in range(KC)], [hbuf], D, n)
                    for kc in range(KC):
                        S.op("dve", lambda h: h.scalar_tensor_tensor(out=hn.ap[:, kc, :], in0=hbuf.ap[:, kc, :], scalar=gcol(layer, 0, kc),
                                                                       in1=rstd.ap[:, :], op0=ALU.mult, op1=ALU.mult),
                             reads=[hbuf, rstd, gains], writes=[hn])
                    pairs = []
                    for kc in range(KC):
                        for half in range(2):
                            pairs.append((hn_l[2 * kc + half][:, c0:c0 + n], hn.ap[half * 64:(half + 1) * 64, kc, :]))
                    S.dma_multi("pool", hn.ch, pairs, reads=[hn], writes=t_hn)
                ph.close()
                hng = gather_rows(hn_l, t_hn, D, rc, "hn")

                GQ = self.dscr("GQ", [HGC, 128, Tq], F32)
                GK = self.dscr("GK", [HGC, 128, Tq], F32)
                GV = self.dscr("GV", [HGC, Tq, 128], F32)
                GZ = self.dscr("GZ", [HGC, Tq, 128], F32)
                GB = self.dscr("GB", [Tq, 2 * HGC], F32)
                OGT = self.dscr("OGT", [HGC, 128, Tq], BF16)
                ph = Phase()
                B = tp_bufs(ph, SQ, wcols=KC * 128)
                hnq = [ph.buf("hnq", [128, KC, SQ], BF16) for _ in range(2)]
                wba = ph.buf("wba", [128, KC * 2 * HGC], BF16)
                dst_, td = lw[f"wgba{j}"]
                S.dma("sp", wba.ch, wba.ap[:, :], dst_[:, :], reads=[td], writes=[wba])
                xbuf = [ph.buf("xbuf", [128, 3 + SQ], F32) for _ in range(HGC * 3)]
                for xb in xbuf:
                    S.op("dve", lambda h: h.memset(xb.ap[:, 0:3], 0.0), writes=[xb])
                acc = ph.buf("acc", [128, SQ], F32)
                yb = ph.buf("yb", [128, SQ], F32)
                rn = ph.buf("rn", [128, SQ], F32)
                yn = [ph.buf("yn", [128, SQ], F32) for _ in range(2)]
                trs = [ph.buf("trs", [128, 128], F32) for _ in range(3)]
                gbs = [ph.buf("gbs", [128, 2 * HGC], F32) for _ in range(2)]
                gtm = [ph.buf("gtm", [128, HGC], F32) for _ in range(4)]
                ui = 0
                for si, (r, c0, s0, nq) in enumerate(seq_tiles):
                    hq = hnq[si % 2]
                    pairs, rds = [], []
                    for kc in range(KC):
                        for half in range(2):
                            g_, tg = hng[2 * kc + half]
                            pairs.append((hq.ap[half * 64:(half + 1) * 64, kc, 0:nq], g_.ap().rearrange("(r x) t -> x r t", r=8)[:, r, c0:c0 + nq]))
                            rds.append(tg)
                    S.dma_multi("pool", hq.ch, pairs, reads=rds, writes=[hq])
                    nblk = 1 if si == 0 else SQ // 128
                    nk = NM if si == 0 else 128

                    def to_token_major(src_t, dstD, hh):
                        nonlocal ui
                        for bq in range(nblk):
                            pt_ = ps[4 + ui % 2]
                            tr = trs[ui % 3]
                            ui += 1
                            S.op("pe", lambda h: h.transpose(pt_.ap[0:nk, 0:128], src_t.ap[:, bq * 128:bq * 128 + nk], ident.ap[:, :]), reads=[src_t, ident], writes=[pt_])
                            S.op("act", lambda h: h.activation(out=tr.ap[0:nk, :], in_=pt_.ap[0:nk, 0:128], func=AF.Copy), reads=[pt_], writes=[tr])
                            S.dma("pool", tr.ch, dstD[hh, s0 + bq * 128:s0 + bq * 128 + nk, :], tr.ap[0:nk, :], reads=[tr])

                    for hh in range(HGC):
                        for ty in range(4):
                            wb = load_ltile(B, f"wgq{j}", (hh * 4 + ty) * 128, KC * 128)
                            p = ps[(hh * 4 + ty) % 4]
                            for kc in range(KC):
                                S.op("pe", lambda h: h.matmul(p.ap[:, 0:nq], wb.ap[:, kc * 128:(kc + 1) * 128], hq.ap[:, kc, 0:nq], start=(kc == 0), stop=(kc == KC - 1)),
                                     reads=[wb, hq], writes=[p])
                            if ty == 3:
                                S.op("act", lambda h: h.activation(out=yb.ap[:, 0:nq], in_=p.ap[:, 0:nq], func=AF.Silu), reads=[p], writes=[yb])
                                to_token_major(yb, GZ, hh)
                                continue
                            xb = xbuf[hh * 3 + ty]
                            S.op("act", lambda h: h.activation(out=xb.ap[:, 3:3 + nq], in_=p.ap[:, 0:nq], func=AF.Copy), reads=[p], writes=[xb])
                            cw0 = gp0 + (hh * 3 + ty) * 4
                            S.op("dve", lambda h: h.tensor_scalar(out=acc.ap[:, 0:nq], in0=xb.ap[:, 3:3 + nq], scalar1=gpar.ap[:, cw0 + 3:cw0 + 4], scalar2=None, op0=ALU.mult),
                                 reads=[xb, gpar], writes=[acc])
                            for tap in (2, 1, 0):
                                S.op("dve", lambda h: h.scalar_tensor_tensor(out=acc.ap[:, 0:nq], in0=xb.ap[:, tap:tap + nq], scalar=gpar.ap[:, cw0 + tap:cw0 + tap + 1],
                                                                               in1=acc.ap[:, 0:nq], op0=ALU.mult, op1=ALU.add), reads=[xb, gpar, acc], writes=[acc])
                            S.op("act", lambda h: h.activation(out=xb.ap[:, 0:3], in_=xb.ap[:, nq:nq + 3], func=AF.Copy), reads=[xb], writes=[xb])
                            S.op("act", lambda h: h.activation(out=yb.ap[:, 0:nq], in_=acc.ap[:, 0:nq], func=AF.Silu), reads=[acc], writes=[yb])
                            if ty == 2:
                                to_token_major(yb, GV, hh)
                                continue
                            s_ = B["sq"][ty]
                            pn = ps[6]
                            S.op("act", lambda h: h.activation(out=s_.ap[:, 0:nq], in_=yb.ap[:, 0:nq], func=AF.Square), reads=[yb], writes=[s_])
                            S.op("pe", lambda h: h.matmul(pn.ap[:, 0:nq], ones_bf.ap[:, :], s_.ap[:, 0:nq], start=True, stop=True), reads=[s_, ones_bf], writes=[pn])
                            S.op("act", lambda h: h.activation(out=rn.ap[:, 0:nq], in_=pn.ap[:, 0:nq], func=AF.Sqrt, scale=1.0, bias=EPS), reads=[pn], writes=[rn])
                            S.op("dve", lambda h: h.reciprocal(out=rn.ap[:, 0:nq], in_=rn.ap[:, 0:nq]), reads=[rn], writes=[rn])
                            y2 = yn[ty]
                            sc_ = (128 ** -0.5) if ty == 0 else 1.0
                            S.op("dve", lambda h: h.scalar_tensor_tensor(out=y2.ap[:, 0:nq], in0=yb.ap[:, 0:nq], scalar=sc_, in1=rn.ap[:, 0:nq], op0=ALU.mult, op1=ALU.mult),
                                 reads=[yb, rn], writes=[y2])
                            S.dma("pool", y2.ch, (GQ if ty == 0 else GK)[hh, :, s0:s0 + nq], y2.ap[:, 0:nq], reads=[y2])
                    for bq in range(nblk):
                        pg = ps[7]
                        for kc in range(KC):
                            S.op("pe", lambda h: h.matmul(pg.ap[0:nk, 0:2 * HGC], hq.ap[:, kc, bq * 128:bq * 128 + nk], wba.ap[:, kc * 2 * HGC:(kc + 1) * 2 * HGC],
                                                          start=(kc == 0), stop=(kc == KC - 1)), reads=[hq, wba], writes=[pg])
                        gb_ = gbs[bq % 2]
                        tt_, ab_, ee_, rr_ = gtm
                        S.op("act", lambda h: h.activation(out=gb_.ap[0:nk, 0:HGC], in_=pg.ap[0:nk, 0:HGC], func=AF.Sigmoid), reads=[pg], writes=[gb_])
                        dto = gp0 + HGC * 12 + HGC
                        S.op("dve", lambda h: h.tensor_tensor(out=tt_.ap[0:nk, :], in0=pg.ap[0:nk, HGC:2 * HGC], in1=gpar.ap[0:nk, dto:dto + HGC], op=ALU.add), reads=[pg, gpar], writes=[tt_])
                        S.op("act", lambda h: h.activation(out=ab_.ap[0:nk, :], in_=tt_.ap[0:nk, :], func=AF.Abs), reads=[tt_], writes=[ab_])
                        S.op("act", lambda h: h.activation(out=ee_.ap[0:nk, :], in_=ab_.ap[0:nk, :], func=AF.Exp, scale=-1.0), reads=[ab_], writes=[ee_])
                        S.op("act", lambda h: h.activation(out=ee_.ap[0:nk, :], in_=ee_.ap[0:nk, :], func=AF.Ln, scale=1.0, bias=1.0), reads=[ee_], writes=[ee_])
                        S.op("dve", lambda h: h.tensor_scalar(out=rr_.ap[0:nk, :], in0=tt_.ap[0:nk, :], scalar1=0.0, scalar2=None, op0=ALU.max), reads=[tt_], writes=[rr_])
                        S.op("dve", lambda h: h.tensor_tensor(out=rr_.ap[0:nk, :], in0=rr_.ap[0:nk, :], in1=ee_.ap[0:nk, :], op=ALU.add), reads=[rr_, ee_], writes=[rr_])
                        S.op("dve", lambda h: h.tensor_tensor(out=gb_.ap[0:nk, HGC:2 * HGC], in0=rr_.ap[0:nk, :], in1=nexpA.ap[0:nk, j * HGC:(j + 1) * HGC], op=ALU.mult),
                             reads=[rr_, nexpA, gb_], writes=[gb_])
                        S.dma("pool", gb_.ch, GB[s0 + bq * 128:s0 + bq * 128 + nk, :], gb_.ap[0:nk, :], reads=[gb_])
                ph.close()
                if os.environ.get("KDBG_STOP", "") == "g2a":
                    return None

                ph = Phase()
                chunks_ = [(0, NM)] + [(NM + b * 128, 128) for b in range(c.SEQ // 128)]

                def fb(name, shape=(128, 128), dt=F32, k=2):
                    return [ph.buf(name, list(shape), dt) for _ in range(k)]

                qT, kT, vv, szz = fb("qT"), fb("kT"), fb("vv"), fb("szz")
                gbt = fb("gbt", (128, 2 * HGC))
                gbc = fb("gbc", k=1)[0]
                colb = fb("colb", (128, 8), k=2)
                dmx, dmn = fb("dmx", k=1)[0], fb("dmn", k=1)[0]
                decS, decTU = fb("decS", k=1)[0], fb("decTU", k=1)[0]
                Mb, MTb, Lb = fb("Mb"), fb("MTb"), fb("Lb")
                Xb = fb("Xb", (128, 256))
                ktr, wTb, qdT, atT, kd, vnew = fb("ktr", k=1)[0], fb("wTb", k=1)[0], fb("qdT", k=1)[0], fb("atT", k=1)[0], fb("kd", k=1)[0], fb("vnew", k=1)[0]
                egcb = fb("egcb", k=1)[0]
                W1b, W2b = fb("W1b", k=1)[0], fb("W2b", k=1)[0]
                Sst = fb("Sst", k=1)[0]
                sqo, og = fb("sqo", k=1)[0], fb("og", k=1)[0]
                ogT = fb("ogT", (128, 128), BF16, 2)
                onw = gpar.ap[:, gp0 + HGC * 12 + 2 * HGC: gp0 + HGC * 12 + 2 * HGC + 128]
                pctr = [0]

                def P():
                    pctr[0] += 1
                    return ps[pctr[0] % 7]

                def mm(out_ap, out_t, lhsT, rhs, rd, start=True, stop=True):
                    S.op("pe", lambda h: h.matmul(out_ap, lhsT, rhs, start=start, stop=stop), reads=rd, writes=[out_t])

                def dve(fn, rd, wr):
                    S.op("dve", fn, reads=rd, writes=wr)

                def act(out, in_, func, rd, wr, **kw):
                    S.op("act", lambda h: h.activation(out=out, in_=in_, func=func, **kw), reads=rd, writes=wr)

                ci = 0
                G2BSTOP = int(os.environ.get("KDBG_G2B", "0"))
                for hh in range(HGC):
                    S.op("dve", lambda h: h.memset(Sst.ap[:, :], 0.0), writes=[Sst])
                    for (s0, nt) in chunks_:
                        q_, k_, v_, z_, g_ = qT[ci % 2], kT[ci % 2], vv[ci % 2], szz[ci % 2], gbt[ci % 2]
                        cb = colb[ci % 2]
                        ci += 1
                        if nt < 128:
                            for t_ in (q_, k_, v_, z_, g_):
                                S.op("pool", lambda h: h.memset(t_.ap[:, :], 0.0), writes=[t_])
                        S.dma("sp", q_.ch, q_.ap[:, 0:nt], GQ[hh, :, s0:s0 + nt], writes=[q_])
                        S.dma("sp", k_.ch, k_.ap[:, 0:nt], GK[hh, :, s0:s0 + nt], writes=[k_])
                        S.dma("sp", v_.ch, v_.ap[0:nt, :], GV[hh, s0:s0 + nt, :], writes=[v_])
                        S.dma("sp", z_.ch, z_.ap[0:nt, :], GZ[hh, s0:s0 + nt, :], writes=[z_])
                        S.dma("sp", g_.ch, g_.ap[0:nt, :], GB[s0:s0 + nt, :], writes=[g_])
                        bcol = g_.ap[:, hh:hh + 1]
                        gcol_ = g_.ap[:, HGC + hh:HGC + hh + 1]
                        dve(lambda h: h.tensor_scalar(out=gbc.ap[:, :], in0=ONESF, scalar1=gcol_, scalar2=None, op0=ALU.mult), [g_, gconst], [gbc])
                        p1 = P()
                        mm(p1.ap[:, 0:1], p1, UT, gcol_, [gconst, g_])
                        act(cb.ap[:, 0:1], p1.ap[:, 0:1], AF.Copy, [p1], [cb])
                        p2 = P()
                        mm(p2.ap[:, 0:128], p2, gbc.ap[:, :], UT, [gbc, gconst])
                        p3 = P()
                        mm(p3.ap[:, 0:1], p3, gbc.ap[:, :], ONESF[:, 0:1], [gbc, gconst])
                        act(cb.ap[:, 1:2], p3.ap[:, 0:1], AF.Copy, [p3], [cb])
                        dve(lambda h: h.tensor_scalar(out=dmx.ap[:, :], in0=p2.ap[:, 0:128], scalar1=cb.ap[:, 0:1], scalar2=0.0, op0=ALU.subtract, op1=ALU.max), [p2, cb], [dmx])
                        dve(lambda h: h.tensor_scalar(out=dmn.ap[:, :], in0=p2.ap[:, 0:128], scalar1=cb.ap[:, 0:1], scalar2=0.0, op0=ALU.subtract, op1=ALU.min), [p2, cb], [dmn])
                        act(egcb.ap[:, :], p2.ap[:, 0:128], AF.Exp, [p2], [egcb])
                        act(dmx.ap[:, :], dmx.ap[:, :], AF.Exp, [dmx], [dmx], scale=-1.0)
                        act(dmn.ap[:, :], dmn.ap[:, :], AF.Exp, [dmn], [dmn])
                        dve(lambda h: h.tensor_tensor(out=decS.ap[:, :], in0=dmx.ap[:, :], in1=SLT, op=ALU.mult), [dmx, gconst], [decS])
                        dve(lambda h: h.tensor_tensor(out=decTU.ap[:, :], in0=dmn.ap[:, :], in1=UT, op=ALU.mult), [dmn, gconst], [decTU])
                        act(cb.ap[:, 2:3], cb.ap[:, 0:1], AF.Exp, [cb], [cb])
                        dve(lambda h: h.tensor_tensor(out=cb.ap[:, 3:4], in0=cb.ap[:, 2:3], in1=bcol, op=ALU.mult), [cb, g_], [cb])
                        act(cb.ap[:, 4:5], cb.ap[:, 0:1], AF.Exp, [cb], [cb], scale=-1.0, bias=cb.ap[:, 1:2])
                        act(cb.ap[:, 5:6], cb.ap[:, 1:2], AF.Exp, [cb], [cb])
                        if G2BSTOP and G2BSTOP <= 1:
                            continue
                        M, MT = Mb[0], MTb[0]
                        p4 = P()
                        mm(p4.ap[:, 0:128], p4, k_.ap[:, :], k_.ap[:, :], [k_])
                        dve(lambda h: h.scalar_tensor_tensor(out=M.ap[:, :], in0=p4.ap[:, 0:128], scalar=bcol, in1=decS.ap[:, :], op0=ALU.mult, op1=ALU.mult), [p4, g_, decS], [M])
                        p5 = P()
                        S.op("pe", lambda h: h.transpose(p5.ap[:, 0:128], M.ap[:, :], ident.ap[:, :]), reads=[M, ident], writes=[p5])
                        act(MT.ap[:, :], p5.ap[:, 0:128], AF.Copy, [p5], [MT])
                        if G2BSTOP and G2BSTOP <= 2:
                            continue
                        p6 = P()
                        S.op("pe", lambda h: h.transpose(p6.ap[:, 0:128], k_.ap[:, :], ident.ap[:, :]), reads=[k_, ident], writes=[p6])
                        act(ktr.ap[:, :], p6.ap[:, 0:128], AF.Copy, [p6], [ktr])
                        X = Xb[0]
                        dve(lambda h: h.tensor_scalar(out=X.ap[:, 0:128], in0=v_.ap[:, :], scalar1=bcol, scalar2=None, op0=ALU.mult), [v_, g_], [X])
                        dve(lambda h: h.tensor_scalar(out=X.ap[:, 128:256], in0=ktr.ap[:, :], scalar1=cb.ap[:, 3:4], scalar2=None, op0=ALU.mult), [ktr, cb], [X])
                        if G2BSTOP and G2BSTOP <= 3:
                            continue
                        Tm, TTm = Mb[1], MTb[1]
                        lo, loT = Lb[0], Lb[1]
                        S.op("pool", lambda h: h.tensor_tensor(out=lo.ap[:, :], in0=M.ap[:, :], in1=gmask.ap[:, 0:128], op=ALU.mult), reads=[M, gmask], writes=[lo])
                        S.op("pool", lambda h: h.tensor_tensor(out=loT.ap[:, :], in0=MT.ap[:, :], in1=gmask.ap[:, 7 * 128:8 * 128], op=ALU.mult), reads=[MT, gmask], writes=[loT])
                        dve(lambda h: h.tensor_tensor(out=Tm.ap[:, :], in0=ident.ap[:, :], in1=lo.ap[:, :], op=ALU.subtract), [ident, lo], [Tm])
                        dve(lambda h: h.tensor_tensor(out=TTm.ap[:, :], in0=ident.ap[:, :], in1=loT.ap[:, :], op=ALU.subtract), [ident, loT], [TTm])
                        for lv in range(1, 7):
                            S.op("pool", lambda h: h.tensor_tensor(out=lo.ap[:, :], in0=M.ap[:, :], in1=gmask.ap[:, lv * 128:(lv + 1) * 128], op=ALU.mult), reads=[M, gmask], writes=[lo])
                            pw2 = P()
                            mm(pw2.ap[:, 0:128], pw2, lo.ap[:, :], TTm.ap[:, :], [lo, TTm])
                            act(W2b.ap[:, :], pw2.ap[:, 0:128], AF.Copy, [pw2], [W2b])
                            pv_ = P()
                            mm(pv_.ap[:, 0:128], pv_, Tm.ap[:, :], W2b.ap[:, :], [Tm, W2b])
                            if lv < 6:
                                S.op("pool", lambda h: h.tensor_tensor(out=loT.ap[:, :], in0=MT.ap[:, :], in1=gmask.ap[:, (7 + lv) * 128:(8 + lv) * 128], op=ALU.mult),
                                     reads=[MT, gmask], writes=[loT])
                                pw1 = P()
                                mm(pw1.ap[:, 0:128], pw1, loT.ap[:, :], Tm.ap[:, :], [loT, Tm])
                                act(W1b.ap[:, :], pw1.ap[:, 0:128], AF.Copy, [pw1], [W1b])
                                pu_ = P()
                                mm(pu_.ap[:, 0:128], pu_, TTm.ap[:, :], W1b.ap[:, :], [TTm, W1b])
                                dve(lambda h: h.tensor_tensor(out=Tm.ap[:, :], in0=Tm.ap[:, :], in1=pu_.ap[:, 0:128], op=ALU.subtract), [Tm, pu_, pv_], [Tm])
                            dve(lambda h: h.tensor_tensor(out=TTm.ap[:, :], in0=TTm.ap[:, :], in1=pv_.ap[:, 0:128], op=ALU.subtract), [TTm, pv_], [TTm])
                        px = P()
                        mm(px.ap[:, 0:256], px, TTm.ap[:, :], X.ap[:, :], [TTm, X])
                        X = Xb[1]
                        act(X.ap[:, :], px.ap[:, 0:256], AF.Copy, [px], [X])
                        if G2BSTOP and G2BSTOP <= 4:
                            continue
                        p7 = P()
                        S.op("pe", lambda h: h.transpose(p7.ap[:, 0:128], X.ap[:, 128:256], ident.ap[:, :]), reads=[X, ident], writes=[p7])
                        act(wTb.ap[:, :], p7.ap[:, 0:128], AF.Copy, [p7], [wTb])
                        dve(lambda h: h.tensor_tensor(out=qdT.ap[:, :], in0=q_.ap[:, :], in1=egcb.ap[:, :], op=ALU.mult), [q_, egcb], [qdT])
                        p8 = P()
                        mm(p8.ap[:, 0:128], p8, k_.ap[:, :], q_.ap[:, :], [k_, q_])
                        dve(lambda h: h.tensor_tensor(out=atT.ap[:, :], in0=p8.ap[:, 0:128], in1=decTU.ap[:, :], op=ALU.mult), [p8, decTU], [atT])
                        dve(lambda h: h.tensor_scalar(out=kd.ap[:, :], in0=ktr.ap[:, :], scalar1=cb.ap[:, 4:5], scalar2=None, op0=ALU.mult), [ktr, cb], [kd])
                        p9 = P()
                        mm(p9.ap[:, 0:128], p9, wTb.ap[:, :], Sst.ap[:, :], [wTb, Sst])
                        dve(lambda h: h.tensor_tensor(out=vnew.ap[:, :], in0=X.ap[:, 0:128], in1=p9.ap[:, 0:128], op=ALU.subtract), [X, p9], [vnew])
                        po = P()
                        mm(po.ap[:, 0:128], po, qdT.ap[:, :], Sst.ap[:, :], [qdT, Sst], start=True, stop=False)
                        mm(po.ap[:, 0:128], po, atT.ap[:, :], vnew.ap[:, :], [atT, vnew], start=False, stop=True)
                        pS = P()
                        mm(pS.ap[:, 0:128], pS, kd.ap[:, :], vnew.ap[:, :], [kd, vnew])
                        dve(lambda h: h.scalar_tensor_tensor(out=Sst.ap[:, :], in0=Sst.ap[:, :], scalar=cb.ap[:, 5:6], in1=pS.ap[:, 0:128], op0=ALU.mult, op1=ALU.add),
                            [Sst, cb, pS], [Sst])
                        if G2BSTOP and G2BSTOP <= 5:
                            continue
                        act(sqo.ap[:, :], po.ap[:, 0:128], AF.Square, [po], [sqo])
                        dve(lambda h: h.tensor_reduce(out=cb.ap[:, 6:7], in_=sqo.ap[:, :], axis=AX.X, op=ALU.add), [sqo], [cb])
                        act(cb.ap[:, 6:7], cb.ap[:, 6:7], AF.Sqrt, [cb], [cb], scale=1.0 / 128, bias=EPS)
                        dve(lambda h: h.reciprocal(out=cb.ap[:, 6:7], in_=cb.ap[:, 6:7]), [cb], [cb])
                        dve(lambda h: h.scalar_tensor_tensor(out=og.ap[:, :], in0=po.ap[:, 0:128], scalar=cb.ap[:, 6:7], in1=onw, op0=ALU.mult, op1=ALU.mult), [po, cb, gpar], [og])
                        dve(lambda h: h.tensor_tensor(out=og.ap[:, :], in0=og.ap[:, :], in1=z_.ap[:, :], op=ALU.mult), [og, z_], [og])
                        pt_ = P()
                        S.op("pe", lambda h: h.transpose(pt_.ap[:, 0:128], og.ap[:, :], ident.ap[:, :]), reads=[og, ident], writes=[pt_])
                        ot_ = ogT[ci % 2]
                        act(ot_.ap[:, :], pt_.ap[:, 0:128], AF.Copy, [pt_], [ot_])
                        S.dma("pool", ot_.ch, OGT[hh, :, s0:s0 + nt], ot_.ap[:, 0:nt], reads=[ot_])
                ph.close()
                if os.environ.get("KDBG_STOP", "") == "g2b":
                    return None

                parts = rs_parts("gdn")
                ph = Phase()
                B = tp_bufs(ph, 8, wcols=HGC * 128)
                wo_pass(ph, B, f"wgo{j}", HGC, OGT, parts)
                ph.close()
                return reduce_scatter(parts)

            for layer in range(L):
                S.new_epoch(epoch_sems[layer])
                mix = None
                if layer in mla_layers:
                    mix = mla_layer(layer)
                if layer in gdn_layers:
                    mix = gdn_layer(layer)
                mlp_phase(layer, mix)

            ph = Phase()
            xin = [ph.buf("yin", [128, D], F32) for i in range(2)]
            xtr = [ph.buf("ytr", [128, KC, 128], F32) for i in range(2)]
            t_out = T(out_ext)
            for b in range(Tc // 128):
                xt, xb = xtr[b % 2], xin[b % 2]
                col0 = NM + b * 128
                S.dma("sp", xt.ch, xt.ap[:, :, :], hview[:, :, col0:col0 + 128], reads=[t_hT], writes=[xt])
                for kc in range(KC):
                    p = ps[kc % 4]
                    S.op("pe", lambda h: h.transpose(p.ap[:, 0:128], xt.ap[:, kc, :], ident.ap[:, :]), reads=[xt, ident], writes=[p])
                    if kc % 2 == 0:
                        S.op("act", lambda h: h.activation(out=xb.ap[:, kc * 128:(kc + 1) * 128], in_=p.ap[:, 0:128], func=AF.Copy),
                             reads=[p], writes=[xb])
                    else:
                        S.op("dve", lambda h: h.tensor_copy(out=xb.ap[:, kc * 128:(kc + 1) * 128], in_=p.ap[:, 0:128]),
                             reads=[p], writes=[xb])
                S.dma("pool", xb.ch, out_ext[b * 128:(b + 1) * 128, :], xb.ap[:, :], reads=[xb], writes=[t_out])
            ph.close()
        return nc


def rope_tables(T_):
    half = 32
    inv = (np.float32(10000.0) ** (-np.arange(0, 64, 2, dtype=np.float32) / np.float32(64))).astype(np.float32)
    ang = (np.arange(T_, dtype=np.float32)[:, None] * inv[None, :]).astype(np.float32)
    cos, sin = np.cos(ang).astype(np.float32), np.sin(ang).astype(np.float32)
    cosT = np.concatenate([cos.T, cos.T], axis=0)
    sinT = np.concatenate([-sin.T, sin.T], axis=0)
    return np.ascontiguousarray(cosT), np.ascontiguousarray(sinT)


def prep_inputs(cfg, inp, mixers="all"):
    c = cfg
    L = c.DEPTH
    f32 = np.float32
    x = np.asarray(inp["x"], f32).reshape(c.SEQ, c.D)
    gains = np.asarray(inp["norm_gains"], f32)
    g_fm = np.concatenate([fm_vec(gains[l, w]) for l in range(L) for w in range(4)], axis=1)
    KCTd = min(64, c.FF // 128)
    common = {"meta": np.asarray(inp["meta_tokens"], f32), "gains": g_fm, "ident": np.eye(128, dtype=f32)}
    per = [dict() for _ in range(NCORES)]
    for l in range(L):
        wu = tile_gathered(np.asarray(inp["mlp_w_up"][l], f32), c.KC)
        wd = tile_gathered(np.asarray(inp["mlp_w_down"][l], f32), KCTd)
        for r in range(NCORES):
            per[r][f"wup{l}"] = wu[r]
            per[r][f"wdn{l}"] = wd[r]
    if mixers in ("all", "mla"):
        nm = (L + 1) // 2
        cosT, sinT = rope_tables(c.T)
        common["ropecos"], common["ropesin"] = cosT, sinT
        common["tri"] = np.triu(np.ones((128, 128), f32))
        common["qkg"] = np.concatenate([fm_vec(np.concatenate([np.asarray(inp["mla_q_norm"][j], f32), np.asarray(inp["mla_kv_norm"][j], f32)])) for j in range(nm)], axis=1)
        sw = np.concatenate([np.arange(32, 64), np.arange(0, 32)])
        HMC, KVR, QR = c.HMC, c.KVR, c.QR
        for j in range(nm):
            wkva = np.asarray(inp["mla_wkv_a"][j], f32)
            pe = wkva[:, KVR:]
            wa = np.concatenate([np.asarray(inp["mla_wq_a"][j], f32), wkva[:, :KVR], pe, pe[:, sw]], axis=1)
            wag = tile_gathered(wa, c.KC)
            wqb = np.asarray(inp["mla_wq_b"][j], f32).reshape(QR, c.HM, 192)
            wkvb = np.asarray(inp["mla_wkv_b"][j], f32).reshape(KVR, c.HM, 256)
            wo = np.asarray(inp["mla_wo"][j], f32)
            for r in range(NCORES):
                per[r][f"wa{j}"] = wag[r]
                hs = slice(r * HMC, (r + 1) * HMC)
                q = wqb[:, hs, :]
                qcols = np.concatenate([np.concatenate([q[:, h, :128], q[:, h, 128:], q[:, h, 128:][:, sw]], axis=1) for h in range(HMC)], axis=1)
                per[r][f"wqb{j}"] = tile_local(qcols, 128)
                kk = wkvb[:, hs, :128].reshape(KVR, HMC * 128)
                per[r][f"wkk{j}"] = tile_local(kk, 128)
                vv = wkvb[:, hs, 128:].reshape(KVR, HMC * 128)
                per[r][f"wkv{j}"] = tile_local(vv, HMC * 128).reshape(-1, min(1024, c.KVR // 128 * HMC * 128))
                per[r][f"wol{j}"] = tile_local(wo[r * HMC * 128:(r + 1) * HMC * 128, :], 128)
    if mixers in ("all", "gdn"):
        ng = L // 2
        HGC, H = c.HGC, c.HG
        KW = H * 128
        ones = np.ones((128, 128), f32)
        common["gconst"] = np.concatenate([np.triu(ones), np.triu(ones, 1), np.tril(ones, -1), ones], axis=1)
        ii = np.arange(128)
        lms = []
        for lv in range(7):
            s_ = 1 << lv
            same = (ii[:, None] // (2 * s_)) == (ii[None, :] // (2 * s_))
            lms.append((same & ((ii[:, None] % (2 * s_)) >= s_) & ((ii[None, :] % (2 * s_)) < s_)).astype(f32))
        common["gmask"] = np.concatenate(lms + [m_.T for m_ in lms], axis=1)
        gpars = [[] for _ in range(NCORES)]
        for j in range(ng):
            wq = np.asarray(inp["gdn_w_qkvz"][j], f32)
            wba = np.asarray(inp["gdn_w_ba"][j], f32)
            cw = np.asarray(inp["gdn_conv_w"][j], f32)
            alog = np.asarray(inp["gdn_a_log"][j], f32)
            dtb = np.asarray(inp["gdn_dt_bias"][j], f32)
            onw = np.asarray(inp["gdn_o_norm"][j], f32)
            wo = np.asarray(inp["gdn_wo"][j], f32)
            for r in range(NCORES):
                cols, cwc = [], []
                for hh in range(HGC):
                    hs = r * HGC + hh
                    for ty in range(4):
                        base = ty * KW + hs * 128
                        cols.append(wq[:, base:base + 128])
                        if ty < 3:
                            cwc.append(cw[:, base:base + 128].T)
                per[r][f"wgq{j}"] = tile_local(np.concatenate(cols, axis=1), 128)
                hsl = np.arange(r * HGC, (r + 1) * HGC)
                per[r][f"wgba{j}"] = tile_local(np.concatenate([wba[:, hsl], wba[:, H + hsl]], axis=1), 2 * HGC)
                per[r][f"wgo{j}"] = tile_local(wo[r * HGC * 128:(r + 1) * HGC * 128, :], 128)
                gpars[r].append(np.concatenate(cwc + [np.broadcast_to(alog[hsl][None, :], (128, HGC)),
                                                      np.broadcast_to(dtb[hsl][None, :], (128, HGC)),
                                                      np.broadcast_to(onw[None, :], (128, 128))], axis=1))
        for r in range(NCORES):
            per[r]["gpar"] = np.concatenate(gpars[r], axis=1)
    maps = []
    for r in range(NCORES):
        m = dict(common)
        m["x"] = x[r * c.Tc:(r + 1) * c.Tc]
        m.update(per[r])
        maps.append(m)
    return maps


def run(cfg, inp, mixers="all", trace=False):
    b = Builder(cfg, mixers=mixers)
    nc = b.build()
    maps = prep_inputs(cfg, inp, mixers)
    maps = [{k: np.ascontiguousarray(m[k], dtype=np.float32).reshape(b.inputs[k]) for k in b.inputs} for m in maps]
    res = run_bass_kernel_spmd(nc, maps, core_ids=list(range(NCORES)), trace=trace)
    out = np.concatenate([res.results[r]["out"] for r in range(NCORES)], axis=0)
    return out.reshape(1, cfg.SEQ, cfg.D).astype(np.float32), res, b


def kernel(**inputs):
    cfg = Cfg()
    out, _, _ = run(cfg, inputs)
    return out
```

```python
import math
import os
from contextlib import ExitStack
import numpy as np
import concourse.bass as bass
import concourse.mybir as mybir
from concourse.bass_utils import run_bass_kernel_spmd

F32 = mybir.dt.float32
BF16 = mybir.dt.bfloat16
AF = mybir.ActivationFunctionType
ALU = mybir.AluOpType
AX = mybir.AxisListType
NCORES = 8
NM = 16
EPS = 1e-6


class Cfg:
    def __init__(self, D=4096, SEQ=16384, DEPTH=4, QR=1536, KVR=512, FF=None):
        self.D, self.SEQ, self.DEPTH, self.QR, self.KVR = D, SEQ, DEPTH, QR, KVR
        self.FF = FF or 4 * D
        self.HM = D // 64
        self.HG = D // 128
        self.HMC = self.HM // NCORES
        self.HGC = self.HG // NCORES
        self.Tc = SEQ // NCORES
        self.TcM = self.Tc + NM
        self.T = SEQ + NM
        self.KC = D // 128
        self.NT = max(1, self.TcM // 258) if self.TcM % 258 == 0 else 1
        self.TT = self.TcM // self.NT
        assert self.TT <= 512
        self.SQ = min(512, self.Tc)
        self.NLAT = QR + KVR + 128
        self.NA = self.NLAT // 128


class T:
    __slots__ = ("ap", "lw", "rd", "_ch", "psum", "owner")

    def __init__(self, ap, psum=False):
        self.ap = ap
        self.lw = None
        self.rd = {}
        self.psum = psum
        self._ch = None
        self.owner = None

    @property
    def ch(self):
        if self._ch is None:
            self._ch = self.owner.alloc_ch()
        return self._ch

    def __getitem__(self, k):
        return self.ap[k]


class DmaCh:
    def __init__(self, sem):
        self.sem = sem
        self.cnt = 0


class Sched:
    def __init__(self, nc, engines, sems):
        self.nc = nc
        self.h = engines
        self.sem = sems
        self.cnt = {n: 0 for n in engines}
        self.seen = {n: {} for n in engines}
        self.chs = []
        self.nins = 0
        self.epoch = 0

    def new_epoch(self, sems):
        self.barrier()
        self.sem = sems
        self.cnt = {n: 0 for n in self.h}
        self.epoch += 1

    def _wait(self, e, tk):
        if tk is None:
            return
        sem, val, src = tk[0], tk[1], tk[2]
        if src is not None and tk[3] < self.epoch:
            return
        if src == e and e == "pe":
            return
        sid = id(sem)
        if self.seen[e].get(sid, 0) >= val:
            return
        self.h[e].wait_ge(sem, val)
        self.seen[e][sid] = val
        self.nins += 1

    def deps(self, e, reads, writes):
        for t in reads:
            self._wait(e, t.lw)
            if t.psum:
                for tk in list(t.rd.values()):
                    if tk[2] != e:
                        self._wait(e, tk)
        for t in writes:
            self._wait(e, t.lw)
            for tk in t.rd.values():
                self._wait(e, tk)

    def _mark(self, tk, reads, writes):
        sid = id(tk[0])
        for t in writes:
            t.lw = tk
            t.rd = {}
        for t in reads:
            t.rd[sid] = tk

    def op(self, e, fn, reads=(), writes=()):
        self.deps(e, reads, writes)
        ins = fn(self.h[e])
        self.cnt[e] += 1
        ins.then_inc(self.sem[e], 1)
        tk = (self.sem[e], self.cnt[e], e, self.epoch)
        self._mark(tk, reads, writes)
        self.nins += 1
        return tk

    def dma(self, q, ch, out, in_, reads=(), writes=(), **kw):
        self.deps(q, reads, writes)
        ins = self.h[q].dma_start(out=out, in_=in_, **kw)
        ch.cnt += 16
        ins.then_inc(ch.sem, 16)
        tk = (ch.sem, ch.cnt, None, None)
        self._mark(tk, reads, writes)
        self.nins += 1
        return tk

    def dma_multi(self, q, ch, pairs, reads=(), writes=(), **kw):
        self.deps(q, reads, writes)
        for out, in_ in pairs:
            ins = self.h[q].dma_start(out=out, in_=in_, **kw)
            ch.cnt += 16
            ins.then_inc(ch.sem, 16)
            self.nins += 1
        tk = (ch.sem, ch.cnt, None, None)
        self._mark(tk, reads, writes)
        return tk

    def newch(self, sem):
        c = DmaCh(sem)
        self.chs.append(c)
        return c

    def barrier(self):
        for e in self.h:
            for o in self.h:
                if o != e and self.cnt[o] > 0:
                    self._wait(e, (self.sem[o], self.cnt[o], o, self.epoch))
            for c in self.chs:
                if c.cnt > 0:
                    self._wait(e, (c.sem, c.cnt, None, None))


def tile_gathered(W, KCT):
    K, N = W.shape
    KH = K // (KCT * 128)
    M = N // 128
    kcr = KCT // 8
    a = W.reshape(KH, 8, kcr, 128, M, 128).transpose(1, 0, 4, 3, 2, 5)
    return np.ascontiguousarray(a).reshape(8, KH * M * 128, kcr * 128)


def tile_local(W, NW=128):
    K, N = W.shape
    KC = K // 128
    M = N // NW
    a = W.reshape(KC, 128, M, NW).transpose(2, 1, 0, 3)
    return np.ascontiguousarray(a).reshape(M * 128, KC * NW)


def fm_vec(g):
    return np.ascontiguousarray(g.reshape(-1, 128).T)


AGMAX = 512 * 1024


class Builder:
    def __init__(self, cfg, mixers="all"):
        self.c = cfg
        self.mixers = mixers
        self.nc = bass.Bass("TRN2", target_bir_lowering=False)
        self.inputs = {}
        self.uid = 0

    def din(self, name, shape, dt=F32):
        t = self.nc.dram_tensor(name, list(shape), dt, kind="ExternalInput")
        self.inputs[name] = tuple(shape)
        return t

    def dscr(self, name, shape, dt):
        self.uid += 1
        return self.nc.dram_tensor(f"{name}_{self.uid}", list(shape), dt)

    def build(self):
        c, nc = self.c, self.nc
        D, KC, TcM, TT, NT, Tc, Tq = c.D, c.KC, c.TcM, c.TT, c.NT, c.Tc, c.T
        L = c.DEPTH
        QRC, KVC, NA, HMC, HGC = c.QR // 128, c.KVR // 128, c.NA, c.HMC, c.HGC
        SQ = c.SQ
        NB = 1 + c.SEQ // 128
        KCTd = min(64, c.FF // 128)
        KHd = (c.FF // 128) // KCTd
        FH = KCTd
        do_mla = self.mixers in ("all", "mla")
        do_gdn = self.mixers in ("all", "gdn")
        mla_layers = [i for i in range(L) if i % 2 == 0] if do_mla else []
        gdn_layers = [i for i in range(L) if i % 2 == 1] if do_gdn else []
        if os.environ.get("KDBG_LAYERS"):
            act_ = [int(v) for v in os.environ["KDBG_LAYERS"].split(",")]
            mla_layers = [i for i in mla_layers if i in act_]
            gdn_layers = [i for i in gdn_layers if i in act_]
        x_in = self.din("x", [Tc, D])
        meta_in = self.din("meta", [NM, D])
        out_ext = nc.dram_tensor("out", [Tc, D], F32, kind="ExternalOutput")
        gains_in = self.din("gains", [128, L * 4 * KC])
        ident_in = self.din("ident", [128, 128])
        wup_in = [self.din(f"wup{i}", [(c.FF // 128) * 128, (KC // 8) * 128]) for i in range(L)]
        wdn_in = [self.din(f"wdn{i}", [KHd * KC * 128, (KCTd // 8) * 128]) for i in range(L)]
        hT = self.dscr("hT", [D, TcM], F32)
        hview = hT.ap().rearrange("(kc p) t -> p kc t", p=128)

        es = ExitStack()
        with es:
            def sem(name):
                return es.enter_context(nc.semaphore(name))

            engs = {"pe": nc.tensor, "act": nc.scalar, "dve": nc.vector, "pool": nc.gpsimd, "sp": nc.sync}
            sems = {n_: sem("s_" + n_) for n_ in engs}
            es.enter_context(nc.Block())
            S = Sched(nc, engs, sems)
            self.S = S
            epoch_sems = [{n_: sem(f"s{ei}_" + n_) for n_ in engs} for ei in range(L)]
            ccsem = sem("ccsem")
            self.cc_cnt = 0
            chpool = [S.newch(sem(f"ch{i}")) for i in range(56)]

            class Phase:
                def __init__(ph):
                    ph.es = ExitStack()
                    ph.chs = []

                def buf(ph, name, shape, dt):
                    self.uid += 1
                    ap = ph.es.enter_context(nc.sbuf_tensor(f"{name}_{self.uid}", list(shape), dt))
                    t = T(ap)
                    t.owner = ph
                    return t

                def alloc_ch(ph):
                    c_ = chpool.pop()
                    ph.chs.append(c_)
                    return c_

                def close(ph):
                    S.barrier()
                    ph.es.close()
                    chpool.extend(ph.chs)

            ps = [T(es.enter_context(nc.psum_tensor(f"ps{i}", [128, 512], F32)), psum=True) for i in range(8)]
            gph = Phase()
            ones_bf = gph.buf("ones_bf", [128, 128], BF16)
            S.op("dve", lambda h: h.memset(ones_bf.ap[:], 1.0), writes=[ones_bf])
            gains = gph.buf("gains_sb", [128, L * 4 * KC], F32)
            S.dma("sp", gains.ch, gains.ap[:], gains_in[:, :], writes=[gains])
            ident = gph.buf("ident_sb", [128, 128], F32)
            S.dma("sp", ident.ch, ident.ap[:], ident_in[:, :], writes=[ident])

            def collective(kind, groups, src, dst, src_t, dst_t):
                S.deps("pool", [src_t], [dst_t])
                op = ALU.add if kind == "ReduceScatter" else ALU.bypass
                ins = nc.gpsimd.collective_compute(kind, op, replica_groups=groups,
                                                   ins=[src.ap().opt()], outs=[dst.ap().opt()])
                self.cc_cnt += 1
                ins.then_inc(ccsem, 1)
                tk = (ccsem, self.cc_cnt, None, None)
                S._mark(tk, [src_t], [dst_t])
                S.nins += 1

            G4 = [[0, 1, 2, 3], [4, 5, 6, 7]]
            G2 = [[0, 4], [1, 5], [2, 6], [3, 7]]
            G8 = [list(range(8))]

            def allgather(src, mid, dst, t_src, t_mid, t_dst):
                collective("AllGather", G4, src, mid, t_src, t_mid)
                collective("AllGather", G2, mid, dst, t_mid, t_dst)

            cast_chs = [chpool.pop() for _ in range(4)]
            self.ncast = 0
            wt = {}

            def cast_dma(dst, src, tdst):
                ch = cast_chs[self.ncast % 4]
                self.ncast += 1
                S.dma("pool", ch, dst, src, writes=[tdst])

            def cast_and_gather(name, src):
                R, Ccols = src.shape
                tpc = max(1, (AGMAX // (Ccols * 2)) // 128)
                ntiles = R // 128
                chunks = []
                for c0 in range(0, ntiles, tpc):
                    nt_ = min(tpc, ntiles - c0)
                    rows = nt_ * 128
                    b_ = self.dscr(f"{name}_b", [rows, Ccols], BF16)
                    m_ = self.dscr(f"{name}_m", [4 * rows, Ccols], BF16)
                    g_ = self.dscr(f"{name}_g", [8 * rows, Ccols], BF16)
                    tb, tm, tg = T(b_), T(m_), T(g_)
                    cast_dma(b_[:, :], src[c0 * 128:c0 * 128 + rows, :], tb)
                    allgather(b_, m_, g_, tb, tm, tg)
                    chunks.append((g_, tg, nt_))
                wt[name] = (chunks, tpc)

            lw = {}

            def local_w(name, shape):
                R, Cc = shape
                if Cc > 2048:
                    a_ = -(-Cc // 2048)
                    R, Cc = R * a_, Cc // a_
                src = self.din(name, [R, Cc])
                dst = self.dscr(name + "_lb", [R, Cc], BF16)
                td = T(dst)
                assert Cc <= 2048
                step = 1024
                for r0 in range(0, R, step):
                    r1 = min(R, r0 + step)
                    cast_dma(dst[r0:r1, :], src[r0:r1, :], td)
                lw[name] = (dst, td)

            for i in mla_layers:
                j = i // 2
                wa_in = self.din(f"wa{j}", [NA * 128, (KC // 8) * 128])
                cast_and_gather(f"wa{j}", wa_in)
                local_w(f"wqb{j}", [HMC * 2 * 128, QRC * 128])
                local_w(f"wkk{j}", [HMC * 128, KVC * 128])
                cwv = min(1024, KVC * HMC * 128)
                local_w(f"wkv{j}", [128 * KVC * HMC * 128 // cwv, cwv])
                local_w(f"wol{j}", [KC * 128, HMC * 128])
                if i == 0:
                    cast_and_gather("wup0", wup_in[0])
                    cast_and_gather("wdn0", wdn_in[0])
            for i in gdn_layers:
                j = i // 2
                local_w(f"wgq{j}", [HGC * 4 * 128, KC * 128])
                local_w(f"wgba{j}", [128, KC * 2 * HGC])
                local_w(f"wgo{j}", [KC * 128, HGC * 128])
            for i in range(L):
                if f"wup{i}" not in wt:
                    cast_and_gather(f"wup{i}", wup_in[i])
                    cast_and_gather(f"wdn{i}", wdn_in[i])
            if mla_layers:
                qkg_in = self.din("qkg", [128, ((L + 1) // 2) * (NA - 1)])
                qkg = gph.buf("qkg_sb", [128, ((L + 1) // 2) * (NA - 1)], F32)
                S.dma("sp", qkg.ch, qkg.ap[:], qkg_in[:, :], writes=[qkg])
                cos_in = self.din("ropecos", [64, Tq])
                sin_in = self.din("ropesin", [64, Tq])
                negms = [gph.buf("negm", [128, HMC], F32) for _ in range((L + 1) // 2)]
                tri_in = self.din("tri", [128, 128])
                tri = gph.buf("tri_sb", [128, 128], BF16)
                trif = gph.buf("trif_sb", [128, 128], F32)
                S.dma("sp", trif.ch, trif.ap[:], tri_in[:, :], writes=[trif])
                S.op("dve", lambda h: h.tensor_copy(out=tri.ap[:], in_=trif.ap[:]), reads=[trif], writes=[tri])

            if gdn_layers:
                ng = L // 2
                gconst_in = self.din("gconst", [128, 4 * 128])
                gconst = gph.buf("gconst_sb", [128, 4 * 128], F32)
                S.dma("sp", gconst.ch, gconst.ap[:], gconst_in[:, :], writes=[gconst])
                gmask_in = self.din("gmask", [128, 14 * 128])
                gmask = gph.buf("gmask_sb", [128, 14 * 128], F32)
                S.dma("sp", gmask.ch, gmask.ap[:], gmask_in[:, :], writes=[gmask])
                NGP = HGC * 12 + 2 * HGC + 128
                gpar_in = self.din("gpar", [128, ng * NGP])
                gpar = gph.buf("gpar_sb", [128, ng * NGP], F32)
                S.dma("sp", gpar.ch, gpar.ap[:], gpar_in[:, :], writes=[gpar])
                nexpA = gph.buf("nexpA", [128, ng * HGC], F32)
                for jj in range(ng):
                    o_ = jj * NGP + HGC * 12
                    S.op("act", lambda h: h.activation(out=nexpA.ap[:, jj * HGC:(jj + 1) * HGC], in_=gpar.ap[:, o_:o_ + HGC], func=AF.Exp), reads=[gpar], writes=[nexpA])
                S.op("dve", lambda h: h.tensor_scalar(out=nexpA.ap[:, :], in0=nexpA.ap[:, :], scalar1=-1.0, scalar2=None, op0=ALU.mult), reads=[nexpA], writes=[nexpA])

            t_hT = T(hT)
            ph = Phase()
            xin = [ph.buf("xin", [128, D], F32) for i in range(2)]
            xtr = [ph.buf("xtr", [128, KC, 128], F32) for i in range(2)]
            blocks = [("meta", 0, NM)] + [("x", b * 128, 128) for b in range(Tc // 128)]
            for bi, (src, r0, nr) in enumerate(blocks):
                xb, xt = xin[bi % 2], xtr[bi % 2]
                srcap = meta_in[0:NM, :] if src == "meta" else x_in[r0:r0 + nr, :]
                S.dma("sp", xb.ch, xb.ap[0:nr, :], srcap, writes=[xb])
                for kc in range(KC):
                    p = ps[kc % 4]
                    S.op("pe", lambda h: h.transpose(p.ap[:, 0:nr], xb.ap[0:nr, kc * 128:(kc + 1) * 128], ident.ap[0:nr, 0:nr]),
                         reads=[xb, ident], writes=[p])
                    if kc % 2 == 0:
                        S.op("act", lambda h: h.activation(out=xt.ap[:, kc, 0:nr], in_=p.ap[:, 0:nr], func=AF.Copy),
                             reads=[p], writes=[xt])
                    else:
                        S.op("dve", lambda h: h.tensor_copy(out=xt.ap[:, kc, 0:nr], in_=p.ap[:, 0:nr]),
                             reads=[p], writes=[xt])
                col0 = 0 if src == "meta" else NM + r0
                S.dma("pool", xt.ch, hview[:, :, col0:col0 + nr], xt.ap[:, :, 0:nr], reads=[xt], writes=[t_hT])
            ph.close()

            n = TT
            self.wctr = 0

            def gcol(layer, which, kc):
                o = (layer * 4 + which) * KC + kc
                return gains.ap[:, o:o + 1]

            def tp_bufs(ph, ncols, wcols=64 * 128):
                B = {}
                B["sq"] = [ph.buf("sq", [128, ncols], BF16) for _ in range(2)]
                B["rstd"] = ph.buf("rstd", [128, ncols], F32)
                B["rtmp"] = [ph.buf("rtmp", [128, ncols], F32) for _ in range(2)]
                B["wbuf"] = [ph.buf("wbuf", [128, wcols], BF16) for _ in range(4)]
                return B

            def load_wtile(B, name, tile, kcr):
                b = self.wctr % 4
                self.wctr += 1
                wb = B["wbuf"][b]
                chunks, tpc = wt[name]
                g_, tg, nt_ = chunks[tile // tpc]
                src = g_.ap().rearrange("(r t p) f -> t p r f", r=8, t=nt_, p=128)[tile % tpc]
                dst = wb.ap[:, 0:8 * kcr * 128].rearrange("p (r f) -> p r f", r=8)
                S.dma("sp", wb.ch, dst, src, reads=[tg], writes=[wb])
                return wb

            def load_ltile(B, name, row0, width):
                b = self.wctr % 4
                self.wctr += 1
                wb = B["wbuf"][b]
                dst_, td = lw[name]
                a_ = width // dst_.shape[1]
                src = dst_.ap().rearrange("(t p a) f -> t p a f", p=128, a=a_)[row0 // 128]
                S.dma("sp", wb.ch, wb.ap[:, 0:width].rearrange("p (a f) -> p a f", a=a_), src, reads=[td], writes=[wb])
                return wb

            def norm_stats(B, chunks, reads, Dn, ncols, rows=128):
                acc = ps[7]
                nch = len(chunks)
                for i, ap in enumerate(chunks):
                    s = B["sq"][i % 2]
                    S.op("act", lambda h: h.activation(out=s.ap[0:rows, 0:ncols], in_=ap, func=AF.Square), reads=reads, writes=[s])
                    S.op("pe", lambda h: h.matmul(acc.ap[:, 0:ncols], ones_bf.ap[0:rows, :], s.ap[0:rows, 0:ncols], start=(i == 0), stop=(i == nch - 1)),
                         reads=[s, ones_bf], writes=[acc])
                rstd = B["rstd"]
                S.op("act", lambda h: h.activation(out=rstd.ap[:, 0:ncols], in_=acc.ap[:, 0:ncols], func=AF.Sqrt, scale=1.0 / Dn, bias=EPS),
                     reads=[acc], writes=[rstd])
                S.op("dve", lambda h: h.reciprocal(out=rstd.ap[:, 0:ncols], in_=rstd.ap[:, 0:ncols]), reads=[rstd], writes=[rstd])

            def mlp_phase(layer, mix=None):
                ph = Phase()
                B = tp_bufs(ph, n)
                hbuf = ph.buf("hbuf", [128, KC, n], F32)
                mbuf = ph.buf("mbuf", [128, KC, n], F32)
                hn = ph.buf("hn", [128, KC, n], BF16)
                ffb = ph.buf("ffb", [128, FH, n], BF16)
                rstd, rtmp = B["rstd"], B["rtmp"]
                for ti in range(NT):
                    c0 = ti * n
                    S.dma("pool", hbuf.ch, hbuf.ap[:, :, :], hview[:, :, c0:c0 + n], reads=[t_hT], writes=[hbuf])
                    if mix is not None:
                        pairs = []
                        rds = []
                        for k2, (mt, tmt) in enumerate(mix):
                            kc, half = k2 // 2, k2 % 2
                            pairs.append((mbuf.ap[half * 64:(half + 1) * 64, kc, :], mt[:, c0:c0 + n]))
                            rds.append(tmt)
                        S.dma_multi("pool", mbuf.ch, pairs, reads=rds, writes=[mbuf])
                        norm_stats(B, [mbuf.ap[:, kc, :] for kc in range(KC)], [mbuf], D, n)
                        for kc in range(KC):
                            r = rtmp[kc % 2]
                            S.op("dve", lambda h: h.scalar_tensor_tensor(out=r.ap[:, :], in0=mbuf.ap[:, kc, :], scalar=gcol(layer, 1, kc),
                                                                           in1=rstd.ap[:, :], op0=ALU.mult, op1=ALU.mult),
                                 reads=[mbuf, rstd, gains], writes=[r])
                            S.op("pool", lambda h: h.tensor_tensor(out=hbuf.ap[:, kc, :], in0=hbuf.ap[:, kc, :], in1=r.ap[:, :], op=ALU.add),
                                 reads=[r, hbuf], writes=[hbuf])
                    norm_stats(B, [hbuf.ap[:, kc, :] for kc in range(KC)], [hbuf], D, n)
                    for kc in range(KC):
                        S.op("dve", lambda h: h.scalar_tensor_tensor(out=hn.ap[:, kc, :], in0=hbuf.ap[:, kc, :], scalar=gcol(layer, 2, kc),
                                                                       in1=rstd.ap[:, :], op0=ALU.mult, op1=ALU.mult),
                             reads=[hbuf, rstd, gains], writes=[hn])
                    pi = 0
                    for kh in range(KHd):
                        for f in range(FH):
                            fc = kh * FH + f
                            wb = load_wtile(B, f"wup{layer}", fc, KC // 8)
                            p = ps[pi % 6]
                            pi += 1
                            for kc in range(KC):
                                S.op("pe", lambda h: h.matmul(p.ap[:, 0:n], wb.ap[:, kc * 128:(kc + 1) * 128], hn.ap[:, kc, :],
                                                              start=(kc == 0), stop=(kc == KC - 1)),
                                     reads=[wb, hn], writes=[p])
                            r = rtmp[fc % 2]
                            S.op("act", lambda h: h.activation(out=r.ap[:, :], in_=p.ap[:, 0:n], func=AF.Relu), reads=[p], writes=[r])
                            S.op("dve", lambda h: h.tensor_tensor(out=ffb.ap[:, f, :], in0=r.ap[:, :], in1=r.ap[:, :], op=ALU.mult),
                                 reads=[r], writes=[ffb])
                        for m in range(KC):
                            wb = load_wtile(B, f"wdn{layer}", kh * KC + m, FH // 8)
                            p = ps[pi % 6]
                            pi += 1
                            for f in range(FH):
                                S.op("pe", lambda h: h.matmul(p.ap[:, 0:n], wb.ap[:, f * 128:(f + 1) * 128], ffb.ap[:, f, :],
                                                              start=(f == 0), stop=(f == FH - 1)),
                                     reads=[wb, ffb], writes=[p])
                            if kh == 0:
                                S.op("act", lambda h: h.activation(out=mbuf.ap[:, m, :], in_=p.ap[:, 0:n], func=AF.Copy),
                                     reads=[p], writes=[mbuf])
                            else:
                                S.op("dve", lambda h: h.tensor_tensor(out=mbuf.ap[:, m, :], in0=mbuf.ap[:, m, :], in1=p.ap[:, 0:n], op=ALU.add),
                                     reads=[p, mbuf], writes=[mbuf])
                    norm_stats(B, [mbuf.ap[:, kc, :] for kc in range(KC)], [mbuf], D, n)
                    for kc in range(KC):
                        r = rtmp[kc % 2]
                        S.op("dve", lambda h: h.scalar_tensor_tensor(out=r.ap[:, :], in0=mbuf.ap[:, kc, :], scalar=gcol(layer, 3, kc),
                                                                       in1=rstd.ap[:, :], op0=ALU.mult, op1=ALU.mult),
                             reads=[mbuf, rstd, gains], writes=[r])
                        S.op("pool", lambda h: h.tensor_tensor(out=hbuf.ap[:, kc, :], in0=hbuf.ap[:, kc, :], in1=r.ap[:, :], op=ALU.add),
                             reads=[r, hbuf], writes=[hbuf])
                    S.dma("pool", hbuf.ch, hview[:, :, c0:c0 + n], hbuf.ap[:, :, :], reads=[hbuf], writes=[t_hT])
                ph.close()

            seq_tiles = [(0, 0, 0, NM)]
            for i in range(c.SEQ // SQ):
                r = (i * SQ) // Tc
                seq_tiles.append((r, NM + (i * SQ) % Tc, NM + i * SQ, SQ))

            def gather_rows(local, t_local, R, rc, name):
                out = []
                for r0 in range(0, R, rc):
                    b_ = local[r0 // rc]
                    m_ = self.dscr(name + "_m", [4 * rc, TcM], BF16)
                    g_ = self.dscr(name + "_g", [8 * rc, TcM], BF16)
                    tm, tg = T(m_), T(g_)
                    allgather(b_, m_, g_, t_local[r0 // rc], tm, tg)
                    out.append((g_, tg))
                return out

            def rs_parts(name):
                parts = []
                for k2 in range(D // 64):
                    p_ = self.dscr(name + "_p", [8 * 64, TcM], F32)
                    o_ = self.dscr(name + "_o", [64, TcM], F32)
                    parts.append((p_, T(p_), o_, T(o_)))
                return parts

            def wo_pass(ph, B, wname, nh, OT, parts):
                otb = [ph.buf("otb", [128, nh, SQ], BF16) for _ in range(2)]
                pst = [ph.buf("pst", [128, SQ], F32) for _ in range(4)]
                pi = 0
                for si, (r, c0, s0, nq) in enumerate(seq_tiles):
                    ob = otb[si % 2]
                    S.dma("pool", ob.ch, ob.ap[:, :, 0:nq], OT.ap()[:, :, s0:s0 + nq].rearrange("h p t -> p h t"), writes=[ob])
                    for m in range(KC):
                        wb = load_ltile(B, wname, m * 128, nh * 128)
                        p = ps[pi % 6]
                        st = pst[pi % 4]
                        pi += 1
                        for hh in range(nh):
                            S.op("pe", lambda h: h.matmul(p.ap[:, 0:nq], wb.ap[:, hh * 128:(hh + 1) * 128], ob.ap[:, hh, 0:nq],
                                                          start=(hh == 0), stop=(hh == nh - 1)), reads=[wb, ob], writes=[p])
                        if pi % 2 == 0:
                            S.op("act", lambda h: h.activation(out=st.ap[:, 0:nq], in_=p.ap[:, 0:nq], func=AF.Copy), reads=[p], writes=[st])
                        else:
                            S.op("dve", lambda h: h.tensor_copy(out=st.ap[:, 0:nq], in_=p.ap[:, 0:nq]), reads=[p], writes=[st])
                        pairs = []
                        ranks = range(8) if si == 0 else [r]
                        for half in range(2):
                            p_ = parts[2 * m + half][0]
                            for rr in ranks:
                                pairs.append((p_[rr * 64:(rr + 1) * 64, c0:c0 + nq], st.ap[half * 64:(half + 1) * 64, 0:nq]))
                        S.dma_multi("pool", st.ch, pairs, reads=[st])
                return

            def reduce_scatter(parts):
                S.barrier()
                mix = []
                for (p_, tp, o_, to) in parts:
                    m_ = self.dscr("rsmid", [4 * 64, TcM], F32)
                    tm = T(m_)
                    collective("ReduceScatter", G2, p_, m_, tp, tm)
                    collective("ReduceScatter", G4, m_, o_, tm, to)
                    mix.append((o_, to))
                return mix

            scale = (128 + 64) ** -0.5

            def mla_layer(layer):
                j = layer // 2
                rc = 64
                lat_l = [self.dscr("latl", [rc, TcM], BF16) for _ in range(NA * 2)]
                t_lat = [T(x_) for x_ in lat_l]
                ph = Phase()
                B = tp_bufs(ph, n)
                hbuf = ph.buf("hbuf", [128, KC, n], F32)
                hn = ph.buf("hn", [128, KC, n], BF16)
                cqf = ph.buf("cqf", [128, NA, n], F32)
                latb = ph.buf("latb", [128, NA, n], BF16)
                rstd = B["rstd"]
                for ti in range(NT):
                    c0 = ti * n
                    S.dma("pool", hbuf.ch, hbuf.ap[:, :, :], hview[:, :, c0:c0 + n], reads=[t_hT], writes=[hbuf])
                    norm_stats(B, [hbuf.ap[:, kc, :] for kc in range(KC)], [hbuf], D, n)
                    for kc in range(KC):
                        S.op("dve", lambda h: h.scalar_tensor_tensor(out=hn.ap[:, kc, :], in0=hbuf.ap[:, kc, :], scalar=gcol(layer, 0, kc),
                                                                       in1=rstd.ap[:, :], op0=ALU.mult, op1=ALU.mult),
                             reads=[hbuf, rstd, gains], writes=[hn])
                    for m in range(NA):
                        wb = load_wtile(B, f"wa{j}", m, KC // 8)
                        p = ps[m % 6]
                        for kc in range(KC):
                            S.op("pe", lambda h: h.matmul(p.ap[:, 0:n], wb.ap[:, kc * 128:(kc + 1) * 128], hn.ap[:, kc, :],
                                                          start=(kc == 0), stop=(kc == KC - 1)), reads=[wb, hn], writes=[p])
                        S.op("act", lambda h: h.activation(out=cqf.ap[:, m, :], in_=p.ap[:, 0:n], func=AF.Copy), reads=[p], writes=[cqf])
                    for (m0, m1, Dn) in ((0, QRC, c.QR), (QRC, QRC + KVC, c.KVR)):
                        norm_stats(B, [cqf.ap[:, m, :] for m in range(m0, m1)], [cqf], Dn, n)
                        for m in range(m0, m1):
                            go = j * (NA - 1) + m
                            S.op("dve", lambda h: h.scalar_tensor_tensor(out=latb.ap[:, m, :], in0=cqf.ap[:, m, :], scalar=qkg.ap[:, go:go + 1],
                                                                           in1=rstd.ap[:, :], op0=ALU.mult, op1=ALU.mult),
                                 reads=[cqf, rstd, qkg], writes=[latb])
                    S.op("dve", lambda h: h.tensor_copy(out=latb.ap[:, NA - 1, :], in_=cqf.ap[:, NA - 1, :]), reads=[cqf], writes=[latb])
                    pairs = []
                    for m in range(NA):
                        for half in range(2):
                            pairs.append((lat_l[2 * m + half][:, c0:c0 + n], latb.ap[half * 64:(half + 1) * 64, m, :]))
                    S.dma_multi("pool", latb.ch, pairs, reads=[latb], writes=t_lat)
                ph.close()
                import os
                STOP = os.environ.get("KDBG_STOP", "")
                if STOP == "a1":
                    return None
                latg = gather_rows(lat_l, t_lat, NA * 128, rc, "lat")
                if STOP == "ag":
                    S.barrier()
                    return None

                def lat_rows(m, half, r, c0, nq):
                    g_, tg = latg[2 * m + half]
                    return g_.ap().rearrange("(r x) t -> x r t", r=8)[:, r, c0:c0 + nq], tg

                QN = self.dscr("QN", [HMC, 128, Tq], BF16)
                QP = self.dscr("QP", [HMC, 64, Tq], BF16)
                KN = self.dscr("KN", [HMC, 128, Tq], BF16)
                KP = self.dscr("KP", [64, Tq], BF16)
                Vs = self.dscr("Vs", [HMC, 128, NB, 128], BF16)
                OT = self.dscr("OT", [HMC, 128, Tq], BF16)
                ph = Phase()
                B = tp_bufs(ph, SQ)
                latq = [ph.buf("latq", [128, NA - 1, SQ], BF16) for _ in range(2)]
                pe_r = [ph.buf("pe_r", [64, 2, SQ], BF16) for _ in range(2)]
                cs = [ph.buf("cs", [64, 2, SQ], F32) for _ in range(2)]
                wv = ph.buf("wv", [128, KVC * HMC * 128], BF16)
                dst_, td = lw[f"wkv{j}"]
                S.dma("sp", wv.ch, wv.ap[:, :], dst_.ap().rearrange("(p a) f -> p (a f)", p=128), reads=[td], writes=[wv])
                kpe2 = ph.buf("kpe2", [128, SQ], F32)
                t1 = [ph.buf("t1", [64, SQ], F32) for _ in range(2)]
                t2 = [ph.buf("t2", [64, SQ], F32) for _ in range(2)]
                stg = [ph.buf("stg", [128, SQ], BF16) for _ in range(4)]
                stp = [ph.buf("stp", [64, SQ], BF16) for _ in range(3)]
                vst = [ph.buf("vst", [128, HMC * 128], BF16) for _ in range(2)]
                tot = ph.buf("tot", [128, SQ], F32)
                mx = ph.buf("mx", [128, 4], F32)
                qmax2 = ph.buf("qmax2", [128, HMC], F32)
                kmax2 = ph.buf("kmax2", [128, HMC], F32)
                S.op("dve", lambda h: h.memset(qmax2.ap[:], 0.0), writes=[qmax2])
                S.op("dve", lambda h: h.memset(kmax2.ap[:], 0.0), writes=[kmax2])
                sg = 0
                blk_ctr = 0
                for si, (r, c0, s0, nq) in enumerate(seq_tiles):
                    lq, pr, csb = latq[si % 2], pe_r[si % 2], cs[si % 2]
                    pairs, rds = [], []
                    for m in range(NA - 1):
                        for half in range(2):
                            ap_, tg = lat_rows(m, half, r, c0, nq)
                            pairs.append((lq.ap[half * 64:(half + 1) * 64, m, 0:nq], ap_))
                            rds.append(tg)
                    S.dma_multi("pool", lq.ch, pairs, reads=rds, writes=[lq])
                    pairs, rds = [], []
                    for half in range(2):
                        ap_, tg = lat_rows(NA - 1, half, r, c0, nq)
                        pairs.append((pr.ap[:, half, 0:nq], ap_))
                        rds.append(tg)
                    S.dma_multi("pool", pr.ch, pairs, reads=rds, writes=[pr])
                    S.dma_multi("sp", csb.ch, [(csb.ap[:, 0, 0:nq], cos_in[:, s0:s0 + nq]), (csb.ap[:, 1, 0:nq], sin_in[:, s0:s0 + nq])], writes=[csb])

                    def rope(src0, src1, rd, dst):
                        a, b_ = t1[sg % 2], t2[sg % 2]
                        S.op("dve", lambda h: h.tensor_tensor(out=a.ap[:, 0:nq], in0=src0, in1=csb.ap[:, 0, 0:nq], op=ALU.mult), reads=rd + [csb], writes=[a])
                        S.op("dve", lambda h: h.tensor_tensor(out=b_.ap[:, 0:nq], in0=src1, in1=csb.ap[:, 1, 0:nq], op=ALU.mult), reads=rd + [csb], writes=[b_])
                        S.op("pool", lambda h: h.tensor_tensor(out=dst.ap[:, 0:nq], in0=a.ap[:, 0:nq], in1=b_.ap[:, 0:nq], op=ALU.add), reads=[a, b_], writes=[dst])

                    def norm2(pieces, rd, extra, acc_t, hh):
                        pn = ps[6]
                        for i_, (ap_, rows) in enumerate(pieces):
                            s = B["sq"][i_ % 2]
                            S.op("act", lambda h: h.activation(out=s.ap[0:rows, 0:nq], in_=ap_, func=AF.Square), reads=rd, writes=[s])
                            S.op("pe", lambda h: h.matmul(pn.ap[:, 0:nq], ones_bf.ap[0:rows, :], s.ap[0:rows, 0:nq], start=(i_ == 0), stop=(i_ == len(pieces) - 1)),
                                 reads=[s, ones_bf], writes=[pn])
                        if extra is not None:
                            S.op("dve", lambda h: h.tensor_tensor(out=tot.ap[:, 0:nq], in0=pn.ap[:, 0:nq], in1=extra.ap[:, 0:nq], op=ALU.add), reads=[pn, extra], writes=[tot])
                            S.op("dve", lambda h: h.reduce_max(out=mx.ap[:, 0:1], in_=tot.ap[:, 0:nq], axis=AX.X), reads=[tot], writes=[mx])
                        else:
                            S.op("dve", lambda h: h.reduce_max(out=mx.ap[:, 0:1], in_=pn.ap[:, 0:nq], axis=AX.X), reads=[pn], writes=[mx])
                        S.op("dve", lambda h: h.tensor_tensor(out=acc_t.ap[:, hh:hh + 1], in0=acc_t.ap[:, hh:hh + 1], in1=mx.ap[:, 0:1], op=ALU.max), reads=[mx, acc_t], writes=[acc_t])

                    kp = stp[2]
                    rope(pr.ap[:, 0, 0:nq], pr.ap[:, 1, 0:nq], [pr], kp)
                    sg += 1
                    S.dma("pool", kp.ch, KP[:, s0:s0 + nq], kp.ap[:, 0:nq], reads=[kp])
                    pk = ps[6]
                    s_ = B["sq"][0]
                    S.op("act", lambda h: h.activation(out=s_.ap[0:64, 0:nq], in_=kp.ap[:, 0:nq], func=AF.Square), reads=[kp], writes=[s_])
                    S.op("pe", lambda h: h.matmul(pk.ap[:, 0:nq], ones_bf.ap[0:64, :], s_.ap[0:64, 0:nq], start=True, stop=True), reads=[s_, ones_bf], writes=[pk])
                    S.op("act", lambda h: h.activation(out=kpe2.ap[:, 0:nq], in_=pk.ap[:, 0:nq], func=AF.Copy), reads=[pk], writes=[kpe2])
                    for hh in range(HMC):
                        wb = load_ltile(B, f"wqb{j}", (2 * hh) * 128, QRC * 128)
                        pq = ps[0]
                        for kc in range(QRC):
                            S.op("pe", lambda h: h.matmul(pq.ap[:, 0:nq], wb.ap[:, kc * 128:(kc + 1) * 128], lq.ap[:, kc, 0:nq], start=(kc == 0), stop=(kc == QRC - 1)),
                                 reads=[wb, lq], writes=[pq])
                        st = stg[(2 * hh) % 4]
                        S.op("act", lambda h: h.activation(out=st.ap[:, 0:nq], in_=pq.ap[:, 0:nq], func=AF.Copy), reads=[pq], writes=[st])
                        S.dma("pool", st.ch, QN[hh, :, s0:s0 + nq], st.ap[:, 0:nq], reads=[st])
                        wb2 = load_ltile(B, f"wqb{j}", (2 * hh + 1) * 128, QRC * 128)
                        pa, pb = ps[1], ps[2]
                        for kc in range(QRC):
                            S.op("pe", lambda h: h.matmul(pa.ap[0:64, 0:nq], wb2.ap[:, kc * 128:kc * 128 + 64], lq.ap[:, kc, 0:nq], start=(kc == 0), stop=(kc == QRC - 1)),
                                 reads=[wb2, lq], writes=[pa])
                        for kc in range(QRC):
                            S.op("pe", lambda h: h.matmul(pb.ap[0:64, 0:nq], wb2.ap[:, kc * 128 + 64:(kc + 1) * 128], lq.ap[:, kc, 0:nq], start=(kc == 0), stop=(kc == QRC - 1)),
                                 reads=[wb2, lq], writes=[pb])
                        qp = stp[hh % 2]
                        rope(pa.ap[0:64, 0:nq], pb.ap[0:64, 0:nq], [pa, pb], qp)
                        sg += 1
                        S.dma("pool", qp.ch, QP[hh, :, s0:s0 + nq], qp.ap[:, 0:nq], reads=[qp])
                        norm2([(st.ap[:, 0:nq], 128), (qp.ap[:, 0:nq], 64)], [st, qp], None, qmax2, hh)
                        wb3 = load_ltile(B, f"wkk{j}", hh * 128, KVC * 128)
                        pkn = ps[3]
                        for kc in range(KVC):
                            S.op("pe", lambda h: h.matmul(pkn.ap[:, 0:nq], wb3.ap[:, kc * 128:(kc + 1) * 128], lq.ap[:, QRC + kc, 0:nq], start=(kc == 0), stop=(kc == KVC - 1)),
                                 reads=[wb3, lq], writes=[pkn])
                        st2 = stg[(2 * hh + 1) % 4]
                        S.op("act", lambda h: h.activation(out=st2.ap[:, 0:nq], in_=pkn.ap[:, 0:nq], func=AF.Copy), reads=[pkn], writes=[st2])
                        S.dma("pool", st2.ch, KN[hh, :, s0:s0 + nq], st2.ap[:, 0:nq], reads=[st2])
                        norm2([(st2.ap[:, 0:nq], 128)], [st2], kpe2, kmax2, hh)
                    nblk = 1 if si == 0 else SQ // 128
                    for bq in range(nblk):
                        nk = NM if si == 0 else 128
                        blk = 0 if si == 0 else 1 + (s0 - NM) // 128 + bq
                        vb = vst[blk_ctr % 2]
                        blk_ctr += 1
                        W_ = HMC * 128
                        for hf in range(0, W_, 512):
                            wd = min(512, W_ - hf)
                            pv = ps[4 + (hf // 512) % 2]
                            for kc in range(KVC):
                                S.op("pe", lambda h: h.matmul(pv.ap[0:nk, 0:wd], lq.ap[:, QRC + kc, bq * 128:bq * 128 + nk], wv.ap[:, kc * W_ + hf:kc * W_ + hf + wd],
                                                              start=(kc == 0), stop=(kc == KVC - 1)), reads=[lq, wv], writes=[pv])
                            S.op("act", lambda h: h.activation(out=vb.ap[0:nk, hf:hf + wd], in_=pv.ap[0:nk, 0:wd], func=AF.Copy), reads=[pv], writes=[vb])
                        S.dma("pool", vb.ch, Vs.ap()[:, 0:nk, blk, :].rearrange("h p d -> p h d"),
                              vb.ap[0:nk, :].rearrange("p (h d) -> p h d", h=HMC), reads=[vb])
                negm = negms[j]
                S.op("dve", lambda h: h.tensor_tensor(out=negm.ap[:, :], in0=qmax2.ap[:, :], in1=kmax2.ap[:, :], op=ALU.mult), reads=[qmax2, kmax2], writes=[negm])
                S.op("act", lambda h: h.activation(out=negm.ap[:, :], in_=negm.ap[:, :], func=AF.Sqrt), reads=[negm], writes=[negm])
                S.op("dve", lambda h: h.tensor_scalar(out=negm.ap[:, :], in0=negm.ap[:, :], scalar1=-scale, scalar2=None, op0=ALU.mult), reads=[negm], writes=[negm])
                ph.close()
                if STOP == "a2a":
                    return None

                ph = Phase()
                kpT = ph.buf("kpT", [64, Tq], BF16)
                S.dma("sp", kpT.ch, kpT.ap[:, :], KP[:, :], writes=[kpT])
                knT = ph.buf("knT", [128, Tq], BF16)
                vT = ph.buf("vT", [128, NB, 128], BF16)
                qn = [ph.buf("qn", [128, SQ], BF16) for _ in range(2)]
                qp = [ph.buf("qp", [64, SQ], BF16) for _ in range(2)]
                pT = [ph.buf("pT", [128, SQ], BF16) for _ in range(3)]
                rl = ph.buf("rl", [128, SQ], F32)
                ob = [ph.buf("ob", [128, SQ], BF16) for _ in range(2)]
                ui = 0
                ci = 0
                SQB = SQ // 128
                for hh in range(HMC):
                    S.dma("sp", knT.ch, knT.ap[:, :], KN[hh, :, :], writes=[knT])
                    S.dma("sp", vT.ch, vT.ap[:, :, :], Vs[hh, :, :, :], writes=[vT])
                    for si, (r, c0, s0, nq) in enumerate(seq_tiles):
                        qa, qb = qn[ci % 2], qp[ci % 2]
                        S.dma("pool", qa.ch, qa.ap[:, 0:nq], QN[hh, :, s0:s0 + nq], writes=[qa])
                        S.dma("pool", qb.ch, qb.ap[:, 0:nq], QP[hh, :, s0:s0 + nq], writes=[qb])
                        o_ps, l_ps = ps[3 + ci % 2], ps[5 + ci % 2]
                        if si == 0:
                            kbs = [(0, NM, 0, 0, NM)]
                        else:
                            jq = (s0 - NM) // 128
                            kbs = [(0, NM, 0, 0, 0)] + [(b, 128, NM + (b - 1) * 128, 0, 0) for b in range(1, jq + 1)]
                            kbs += [(jq + 1 + d, 128, NM + (jq + d) * 128, 128 * d, 128) for d in range(SQB)]
                        for ki, (blk, nk, k0, clo, dw) in enumerate(kbs):
                            sp_ = ps[ui % 3]
                            pt = pT[ui % 3]
                            ui += 1
                            S.op("pe", lambda h: h.matmul(sp_.ap[0:nk, clo:nq], knT.ap[:, k0:k0 + nk], qa.ap[:, clo:nq], start=True, stop=False),
                                 reads=[knT, qa], writes=[sp_])
                            S.op("pe", lambda h: h.matmul(sp_.ap[0:nk, clo:nq], kpT.ap[:, k0:k0 + nk], qb.ap[:, clo:nq], start=False, stop=True),
                                 reads=[kpT, qb], writes=[sp_])
                            S.op("act", lambda h: h.activation(out=pt.ap[0:nk, clo:nq], in_=sp_.ap[0:nk, clo:nq], func=AF.Exp, scale=scale,
                                                               bias=negm.ap[0:nk, hh:hh + 1]), reads=[sp_, negm], writes=[pt])
                            if dw:
                                S.op("pool", lambda h: h.tensor_tensor(out=pt.ap[0:nk, clo:clo + dw], in0=pt.ap[0:nk, clo:clo + dw], in1=tri.ap[0:nk, 0:dw], op=ALU.mult),
                                     reads=[pt, tri], writes=[pt])
                            first, last = (ki == 0), (ki == len(kbs) - 1)
                            S.op("pe", lambda h: h.matmul(o_ps.ap[:, clo:nq], vT.ap[0:nk, blk, :], pt.ap[0:nk, clo:nq], start=first, stop=last),
                                 reads=[vT, pt], writes=[o_ps])
                            S.op("pe", lambda h: h.matmul(l_ps.ap[:, clo:nq], ones_bf.ap[0:nk, :], pt.ap[0:nk, clo:nq], start=first, stop=last),
                                 reads=[ones_bf, pt], writes=[l_ps])
                        S.op("dve", lambda h: h.reciprocal(out=rl.ap[:, 0:nq], in_=l_ps.ap[:, 0:nq]), reads=[l_ps], writes=[rl])
                        o_ = ob[ci % 2]
                        S.op("dve", lambda h: h.tensor_tensor(out=o_.ap[:, 0:nq], in0=o_ps.ap[:, 0:nq], in1=rl.ap[:, 0:nq], op=ALU.mult), reads=[o_ps, rl], writes=[o_])
                        S.dma("pool", o_.ch, OT[hh, :, s0:s0 + nq], o_.ap[:, 0:nq], reads=[o_])
                        ci += 1
                ph.close()
                if STOP == "attn":
                    return None

                parts = rs_parts("mla")
                ph = Phase()
                B = tp_bufs(ph, 8)
                wo_pass(ph, B, f"wol{j}", HMC, OT, parts)
                ph.close()
                if STOP == "wo":
                    return None
                return reduce_scatter(parts)

            def gdn_layer(layer):
                j = layer // 2
                NGP = HGC * 12 + 2 * HGC + 128
                gp0 = j * NGP
                UT = gconst.ap[:, 0:128]
                SUT = gconst.ap[:, 128:256]
                SLT = gconst.ap[:, 256:384]
                ONESF = gconst.ap[:, 384:512]
                rc = 64
                hn_l = [self.dscr("hnl", [rc, TcM], BF16) for _ in range(D // rc)]
                t_hn = [T(x_) for x_ in hn_l]
                ph = Phase()
                B = tp_bufs(ph, n, wcols=128)
                hbuf = ph.buf("hbuf", [128, KC, n], F32)
                hn = ph.buf("hn", [128, KC, n], BF16)
                rstd = B["rstd"]
                for ti in range(NT):
                    c0 = ti * n
                    S.dma("pool", hbuf.ch, hbuf.ap[:, :, :], hview[:, :, c0:c0 + n], reads=[t_hT], writes=[hbuf])
                    norm_stats(B, [hbuf.ap[:, kc, :] for kc in range(KC)], [hbuf], D, n)
                    for kc in range(KC):
                        S.op("dve", lambda h: h.scalar_tensor_tensor(out=hn.ap[:, kc, :], in0=hbuf.ap[:, kc, :], scalar=gcol(layer, 0, kc),
                                                                       in1=rstd.ap[:, :], op0=ALU.mult, op1=ALU.mult),
                             reads=[hbuf, rstd, gains], writes=[hn])
                    pairs = []
                    for kc in range(KC):
                        for half in range(2):
                            pairs.append((hn_l[2 * kc + half][:, c0:c0 + n], hn.ap[half * 64:(half + 1) * 64, kc, :]))
                    S.dma_multi("pool", hn.ch, pairs, reads=[hn], writes=t_hn)
                ph.close()
                hng = gather_rows(hn_l, t_hn, D, rc, "hn")

                GQ = self.dscr("GQ", [HGC, 128, Tq], F32)
                GK = self.dscr("GK", [HGC, 128, Tq], F32)
                GV = self.dscr("GV", [HGC, Tq, 128], F32)
                GZ = self.dscr("GZ", [HGC, Tq, 128], F32)
                GB = self.dscr("GB", [Tq, 2 * HGC], F32)
                OGT = self.dscr("OGT", [HGC, 128, Tq], BF16)
                ph = Phase()
                B = tp_bufs(ph, SQ, wcols=KC * 128)
                hnq = [ph.buf("hnq", [128, KC, SQ], BF16) for _ in range(2)]
                wba = ph.buf("wba", [128, KC * 2 * HGC], BF16)
                dst_, td = lw[f"wgba{j}"]
                S.dma("sp", wba.ch, wba.ap[:, :], dst_[:, :], reads=[td], writes=[wba])
                xbuf = [ph.buf("xbuf", [128, 3 + SQ], F32) for _ in range(HGC * 3)]
                for xb in xbuf:
                    S.op("dve", lambda h: h.memset(xb.ap[:, 0:3], 0.0), writes=[xb])
                acc = ph.buf("acc", [128, SQ], F32)
                yb = ph.buf("yb", [128, SQ], F32)
                rn = ph.buf("rn", [128, SQ], F32)
                yn = [ph.buf("yn", [128, SQ], F32) for _ in range(2)]
                trs = [ph.buf("trs", [128, 128], F32) for _ in range(3)]
                gbs = [ph.buf("gbs", [128, 2 * HGC], F32) for _ in range(2)]
                gtm = [ph.buf("gtm", [128, HGC], F32) for _ in range(4)]
                ui = 0
                for si, (r, c0, s0, nq) in enumerate(seq_tiles):
                    hq = hnq[si % 2]
                    pairs, rds = [], []
                    for kc in range(KC):
                        for half in range(2):
                            g_, tg = hng[2 * kc + half]
                            pairs.append((hq.ap[half * 64:(half + 1) * 64, kc, 0:nq], g_.ap().rearrange("(r x) t -> x r t", r=8)[:, r, c0:c0 + nq]))
                            rds.append(tg)
                    S.dma_multi("pool", hq.ch, pairs, reads=rds, writes=[hq])
                    nblk = 1 if si == 0 else SQ // 128
                    nk = NM if si == 0 else 128

                    def to_token_major(src_t, dstD, hh):
                        nonlocal ui
                        for bq in range(nblk):
                            pt_ = ps[4 + ui % 2]
                            tr = trs[ui % 3]
                            ui += 1
                            S.op("pe", lambda h: h.transpose(pt_.ap[0:nk, 0:128], src_t.ap[:, bq * 128:bq * 128 + nk], ident.ap[:, :]), reads=[src_t, ident], writes=[pt_])
                            S.op("act", lambda h: h.activation(out=tr.ap[0:nk, :], in_=pt_.ap[0:nk, 0:128], func=AF.Copy), reads=[pt_], writes=[tr])
                            S.dma("pool", tr.ch, dstD[hh, s0 + bq * 128:s0 + bq * 128 + nk, :], tr.ap[0:nk, :], reads=[tr])

                    for hh in range(HGC):
                        for ty in range(4):
                            wb = load_ltile(B, f"wgq{j}", (hh * 4 + ty) * 128, KC * 128)
                            p = ps[(hh * 4 + ty) % 4]
                            for kc in range(KC):
                                S.op("pe", lambda h: h.matmul(p.ap[:, 0:nq], wb.ap[:, kc * 128:(kc + 1) * 128], hq.ap[:, kc, 0:nq], start=(kc == 0), stop=(kc == KC - 1)),
                                     reads=[wb, hq], writes=[p])
                            if ty == 3:
                                S.op("act", lambda h: h.activation(out=yb.ap[:, 0:nq], in_=p.ap[:, 0:nq], func=AF.Silu), reads=[p], writes=[yb])
                                to_token_major(yb, GZ, hh)
                                continue
                            xb = xbuf[hh * 3 + ty]
                            S.op("act", lambda h: h.activation(out=xb.ap[:, 3:3 + nq], in_=p.ap[:, 0:nq], func=AF.Copy), reads=[p], writes=[xb])
                            cw0 = gp0 + (hh * 3 + ty) * 4
                            S.op("dve", lambda h: h.tensor_scalar(out=acc.ap[:, 0:nq], in0=xb.ap[:, 3:3 + nq], scalar1=gpar.ap[:, cw0 + 3:cw0 + 4], scalar2=None, op0=ALU.mult),
                                 reads=[xb, gpar], writes=[acc])
                            for tap in (2, 1, 0):
                                S.op("dve", lambda h: h.scalar_tensor_tensor(out=acc.ap[:, 0:nq], in0=xb.ap[:, tap:tap + nq], scalar=gpar.ap[:, cw0 + tap:cw0 + tap + 1],
                                                                               in1=acc.ap[:, 0:nq], op0=ALU.mult, op1=ALU.add), reads=[xb, gpar, acc], writes=[acc])
                            S.op("act", lambda h: h.activation(out=xb.ap[:, 0:3], in_=xb.ap[:, nq:nq + 3], func=AF.Copy), reads=[xb], writes=[xb])
                            S.op("act", lambda h: h.activation(out=yb.ap[:, 0:nq], in_=acc.ap[:, 0:nq], func=AF.Silu), reads=[acc], writes=[yb])
                            if ty == 2:
                                to_token_major(yb, GV, hh)
                                continue
                            s_ = B["sq"][ty]
                            pn = ps[6]
                            S.op("act", lambda h: h.activation(out=s_.ap[:, 0:nq], in_=yb.ap[:, 0:nq], func=AF.Square), reads=[yb], writes=[s_])
                            S.op("pe", lambda h: h.matmul(pn.ap[:, 0:nq], ones_bf.ap[:, :], s_.ap[:, 0:nq], start=True, stop=True), reads=[s_, ones_bf], writes=[pn])
                            S.op("act", lambda h: h.activation(out=rn.ap[:, 0:nq], in_=pn.ap[:, 0:nq], func=AF.Sqrt, scale=1.0, bias=EPS), reads=[pn], writes=[rn])
                            S.op("dve", lambda h: h.reciprocal(out=rn.ap[:, 0:nq], in_=rn.ap[:, 0:nq]), reads=[rn], writes=[rn])
                            y2 = yn[ty]
                            sc_ = (128 ** -0.5) if ty == 0 else 1.0
                            S.op("dve", lambda h: h.scalar_tensor_tensor(out=y2.ap[:, 0:nq], in0=yb.ap[:, 0:nq], scalar=sc_, in1=rn.ap[:, 0:nq], op0=ALU.mult, op1=ALU.mult),
                                 reads=[yb, rn], writes=[y2])
                            S.dma("pool", y2.ch, (GQ if ty == 0 else GK)[hh, :, s0:s0 + nq], y2.ap[:, 0:nq], reads=[y2])
                    for bq in range(nblk):
                        pg = ps[7]
                        for kc in range(KC):
                            S.op("pe", lambda h: h.matmul(pg.ap[0:nk, 0:2 * HGC], hq.ap[:, kc, bq * 128:bq * 128 + nk], wba.ap[:, kc * 2 * HGC:(kc + 1) * 2 * HGC],
                                                          start=(kc == 0), stop=(kc == KC - 1)), reads=[hq, wba], writes=[pg])
                        gb_ = gbs[bq % 2]
                        tt_, ab_, ee_, rr_ = gtm
                        S.op("act", lambda h: h.activation(out=gb_.ap[0:nk, 0:HGC], in_=pg.ap[0:nk, 0:HGC], func=AF.Sigmoid), reads=[pg], writes=[gb_])
                        dto = gp0 + HGC * 12 + HGC
                        S.op("dve", lambda h: h.tensor_tensor(out=tt_.ap[0:nk, :], in0=pg.ap[0:nk, HGC:2 * HGC], in1=gpar.ap[0:nk, dto:dto + HGC], op=ALU.add), reads=[pg, gpar], writes=[tt_])
                        S.op("act", lambda h: h.activation(out=ab_.ap[0:nk, :], in_=tt_.ap[0:nk, :], func=AF.Abs), reads=[tt_], writes=[ab_])
                        S.op("act", lambda h: h.activation(out=ee_.ap[0:nk, :], in_=ab_.ap[0:nk, :], func=AF.Exp, scale=-1.0), reads=[ab_], writes=[ee_])
                        S.op("act", lambda h: h.activation(out=ee_.ap[0:nk, :], in_=ee_.ap[0:nk, :], func=AF.Ln, scale=1.0, bias=1.0), reads=[ee_], writes=[ee_])
                        S.op("dve", lambda h: h.tensor_scalar(out=rr_.ap[0:nk, :], in0=tt_.ap[0:nk, :], scalar1=0.0, scalar2=None, op0=ALU.max), reads=[tt_], writes=[rr_])
                        S.op("dve", lambda h: h.tensor_tensor(out=rr_.ap[0:nk, :], in0=rr_.ap[0:nk, :], in1=ee_.ap[0:nk, :], op=ALU.add), reads=[rr_, ee_], writes=[rr_])
                        S.op("dve", lambda h: h.tensor_tensor(out=gb_.ap[0:nk, HGC:2 * HGC], in0=rr_.ap[0:nk, :], in1=nexpA.ap[0:nk, j * HGC:(j + 1) * HGC], op=ALU.mult),
                             reads=[rr_, nexpA, gb_], writes=[gb_])
                        S.dma("pool", gb_.ch, GB[s0 + bq * 128:s0 + bq * 128 + nk, :], gb_.ap[0:nk, :], reads=[gb_])
                ph.close()
                if os.environ.get("KDBG_STOP", "") == "g2a":
                    return None

                ph = Phase()
                chunks_ = [(0, NM)] + [(NM + b * 128, 128) for b in range(c.SEQ // 128)]

                def fb(name, shape=(128, 128), dt=F32, k=2):
                    return [ph.buf(name, list(shape), dt) for _ in range(k)]

                def head_gen(hh):
                    qT, kT, vv, szz = fb("qT"), fb("kT"), fb("vv"), fb("szz")
                    gbt = fb("gbt", (128, 2 * HGC))
                    gbc = fb("gbc", k=1)[0]
                    colb = fb("colb", (128, 8), k=2)
                    dmx, dmn = fb("dmx", k=1)[0], fb("dmn", k=1)[0]
                    decS, decTU = fb("decS", k=1)[0], fb("decTU", k=1)[0]
                    Mb, MTb, Lb = fb("Mb"), fb("MTb"), fb("Lb")
                    Xb = fb("Xb", (128, 256))
                    ktr, wTb, qdT, atT, kd, vnew = fb("ktr", k=1)[0], fb("wTb", k=1)[0], fb("qdT", k=1)[0], fb("atT", k=1)[0], fb("kd", k=1)[0], fb("vnew", k=1)[0]
                    egcb = fb("egcb", k=1)[0]
                    W1b, W2b = fb("W1b", k=1)[0], fb("W2b", k=1)[0]
                    Sst = fb("Sst", k=1)[0]
                    sqo, og = fb("sqo", k=1)[0], fb("og", k=1)[0]
                    ogT = fb("ogT", (128, 128), BF16, 2)
                    onw = gpar.ap[:, gp0 + HGC * 12 + 2 * HGC: gp0 + HGC * 12 + 2 * HGC + 128]
                    pctr = [0]
                    pbase = 4 * (hh % 2)

                    def P():
                        pctr[0] += 1
                        return ps[pbase + pctr[0] % 4]

                    def mm(out_ap, out_t, lhsT, rhs, rd, start=True, stop=True):
                        S.op("pe", lambda h: h.matmul(out_ap, lhsT, rhs, start=start, stop=stop), reads=rd, writes=[out_t])

                    def dve(fn, rd, wr):
                        S.op("dve", fn, reads=rd, writes=wr)

                    def act(out, in_, func, rd, wr, **kw):
                        S.op("act", lambda h: h.activation(out=out, in_=in_, func=func, **kw), reads=rd, writes=wr)

                    ci = 0
                    G2BSTOP = 0
                    if True:
                        S.op("dve", lambda h: h.memset(Sst.ap[:, :], 0.0), writes=[Sst])
                        for (s0, nt) in chunks_:
                            q_, k_, v_, z_, g_ = qT[ci % 2], kT[ci % 2], vv[ci % 2], szz[ci % 2], gbt[ci % 2]
                            cb = colb[ci % 2]
                            ci += 1
                            if nt < 128:
                                for t_ in (q_, k_, v_, z_, g_):
                                    S.op("pool", lambda h: h.memset(t_.ap[:, :], 0.0), writes=[t_])
                            S.dma("sp", q_.ch, q_.ap[:, 0:nt], GQ[hh, :, s0:s0 + nt], writes=[q_])
                            S.dma("sp", k_.ch, k_.ap[:, 0:nt], GK[hh, :, s0:s0 + nt], writes=[k_])
                            S.dma("sp", v_.ch, v_.ap[0:nt, :], GV[hh, s0:s0 + nt, :], writes=[v_])
                            S.dma("sp", z_.ch, z_.ap[0:nt, :], GZ[hh, s0:s0 + nt, :], writes=[z_])
                            S.dma("sp", g_.ch, g_.ap[0:nt, :], GB[s0:s0 + nt, :], writes=[g_])
                            bcol = g_.ap[:, hh:hh + 1]
                            gcol_ = g_.ap[:, HGC + hh:HGC + hh + 1]
                            dve(lambda h: h.tensor_scalar(out=gbc.ap[:, :], in0=ONESF, scalar1=gcol_, scalar2=None, op0=ALU.mult), [g_, gconst], [gbc])
                            p1 = P()
                            mm(p1.ap[:, 0:1], p1, UT, gcol_, [gconst, g_])
                            act(cb.ap[:, 0:1], p1.ap[:, 0:1], AF.Copy, [p1], [cb])
                            p2 = P()
                            mm(p2.ap[:, 0:128], p2, gbc.ap[:, :], UT, [gbc, gconst])
                            p3 = P()
                            mm(p3.ap[:, 0:1], p3, gbc.ap[:, :], ONESF[:, 0:1], [gbc, gconst])
                            act(cb.ap[:, 1:2], p3.ap[:, 0:1], AF.Copy, [p3], [cb])
                            dve(lambda h: h.tensor_scalar(out=dmx.ap[:, :], in0=p2.ap[:, 0:128], scalar1=cb.ap[:, 0:1], scalar2=0.0, op0=ALU.subtract, op1=ALU.max), [p2, cb], [dmx])
                            dve(lambda h: h.tensor_scalar(out=dmn.ap[:, :], in0=p2.ap[:, 0:128], scalar1=cb.ap[:, 0:1], scalar2=0.0, op0=ALU.subtract, op1=ALU.min), [p2, cb], [dmn])
                            act(egcb.ap[:, :], p2.ap[:, 0:128], AF.Exp, [p2], [egcb])
                            act(dmx.ap[:, :], dmx.ap[:, :], AF.Exp, [dmx], [dmx], scale=-1.0)
                            act(dmn.ap[:, :], dmn.ap[:, :], AF.Exp, [dmn], [dmn])
                            dve(lambda h: h.tensor_tensor(out=decS.ap[:, :], in0=dmx.ap[:, :], in1=SLT, op=ALU.mult), [dmx, gconst], [decS])
                            dve(lambda h: h.tensor_tensor(out=decTU.ap[:, :], in0=dmn.ap[:, :], in1=UT, op=ALU.mult), [dmn, gconst], [decTU])
                            act(cb.ap[:, 2:3], cb.ap[:, 0:1], AF.Exp, [cb], [cb])
                            dve(lambda h: h.tensor_tensor(out=cb.ap[:, 3:4], in0=cb.ap[:, 2:3], in1=bcol, op=ALU.mult), [cb, g_], [cb])
                            act(cb.ap[:, 4:5], cb.ap[:, 0:1], AF.Exp, [cb], [cb], scale=-1.0, bias=cb.ap[:, 1:2])
                            act(cb.ap[:, 5:6], cb.ap[:, 1:2], AF.Exp, [cb], [cb])
                            if G2BSTOP and G2BSTOP <= 1:
                                continue
                            yield
                            M, MT = Mb[0], MTb[0]
                            p4 = P()
                            mm(p4.ap[:, 0:128], p4, k_.ap[:, :], k_.ap[:, :], [k_])
                            dve(lambda h: h.scalar_tensor_tensor(out=M.ap[:, :], in0=p4.ap[:, 0:128], scalar=bcol, in1=decS.ap[:, :], op0=ALU.mult, op1=ALU.mult), [p4, g_, decS], [M])
                            p5 = P()
                            S.op("pe", lambda h: h.transpose(p5.ap[:, 0:128], M.ap[:, :], ident.ap[:, :]), reads=[M, ident], writes=[p5])
                            act(MT.ap[:, :], p5.ap[:, 0:128], AF.Copy, [p5], [MT])
                            if G2BSTOP and G2BSTOP <= 2:
                                continue
                            yield
                            p6 = P()
                            S.op("pe", lambda h: h.transpose(p6.ap[:, 0:128], k_.ap[:, :], ident.ap[:, :]), reads=[k_, ident], writes=[p6])
                            act(ktr.ap[:, :], p6.ap[:, 0:128], AF.Copy, [p6], [ktr])
                            X = Xb[0]
                            dve(lambda h: h.tensor_scalar(out=X.ap[:, 0:128], in0=v_.ap[:, :], scalar1=bcol, scalar2=None, op0=ALU.mult), [v_, g_], [X])
                            dve(lambda h: h.tensor_scalar(out=X.ap[:, 128:256], in0=ktr.ap[:, :], scalar1=cb.ap[:, 3:4], scalar2=None, op0=ALU.mult), [ktr, cb], [X])
                            if G2BSTOP and G2BSTOP <= 3:
                                continue
                            yield
                            Tm, TTm = Mb[1], MTb[1]
                            lo, loT = Lb[0], Lb[1]
                            S.op("pool", lambda h: h.tensor_tensor(out=lo.ap[:, :], in0=M.ap[:, :], in1=gmask.ap[:, 0:128], op=ALU.mult), reads=[M, gmask], writes=[lo])
                            S.op("pool", lambda h: h.tensor_tensor(out=loT.ap[:, :], in0=MT.ap[:, :], in1=gmask.ap[:, 7 * 128:8 * 128], op=ALU.mult), reads=[MT, gmask], writes=[loT])
                            dve(lambda h: h.tensor_tensor(out=Tm.ap[:, :], in0=ident.ap[:, :], in1=lo.ap[:, :], op=ALU.subtract), [ident, lo], [Tm])
                            dve(lambda h: h.tensor_tensor(out=TTm.ap[:, :], in0=ident.ap[:, :], in1=loT.ap[:, :], op=ALU.subtract), [ident, loT], [TTm])
                            for lv in range(1, 7):
                                yield
                                S.op("pool", lambda h: h.tensor_tensor(out=lo.ap[:, :], in0=M.ap[:, :], in1=gmask.ap[:, lv * 128:(lv + 1) * 128], op=ALU.mult), reads=[M, gmask], writes=[lo])
                                pw2 = P()
                                mm(pw2.ap[:, 0:128], pw2, lo.ap[:, :], TTm.ap[:, :], [lo, TTm])
                                act(W2b.ap[:, :], pw2.ap[:, 0:128], AF.Copy, [pw2], [W2b])
                                pv_ = P()
                                mm(pv_.ap[:, 0:128], pv_, Tm.ap[:, :], W2b.ap[:, :], [Tm, W2b])
                                if lv < 6:
                                    S.op("pool", lambda h: h.tensor_tensor(out=loT.ap[:, :], in0=MT.ap[:, :], in1=gmask.ap[:, (7 + lv) * 128:(8 + lv) * 128], op=ALU.mult),
                                         reads=[MT, gmask], writes=[loT])
                                    pw1 = P()
                                    mm(pw1.ap[:, 0:128], pw1, loT.ap[:, :], Tm.ap[:, :], [loT, Tm])
                                    act(W1b.ap[:, :], pw1.ap[:, 0:128], AF.Copy, [pw1], [W1b])
                                    pu_ = P()
                                    mm(pu_.ap[:, 0:128], pu_, TTm.ap[:, :], W1b.ap[:, :], [TTm, W1b])
                                    dve(lambda h: h.tensor_tensor(out=Tm.ap[:, :], in0=Tm.ap[:, :], in1=pu_.ap[:, 0:128], op=ALU.subtract), [Tm, pu_, pv_], [Tm])
                                dve(lambda h: h.tensor_tensor(out=TTm.ap[:, :], in0=TTm.ap[:, :], in1=pv_.ap[:, 0:128], op=ALU.subtract), [TTm, pv_], [TTm])
                            px = P()
                            mm(px.ap[:, 0:256], px, TTm.ap[:, :], X.ap[:, :], [TTm, X])
                            X = Xb[1]
                            act(X.ap[:, :], px.ap[:, 0:256], AF.Copy, [px], [X])
                            if G2BSTOP and G2BSTOP <= 4:
                                continue
                            yield
                            p7 = P()
                            S.op("pe", lambda h: h.transpose(p7.ap[:, 0:128], X.ap[:, 128:256], ident.ap[:, :]), reads=[X, ident], writes=[p7])
                            act(wTb.ap[:, :], p7.ap[:, 0:128], AF.Copy, [p7], [wTb])
                            dve(lambda h: h.tensor_tensor(out=qdT.ap[:, :], in0=q_.ap[:, :], in1=egcb.ap[:, :], op=ALU.mult), [q_, egcb], [qdT])
                            p8 = P()
                            mm(p8.ap[:, 0:128], p8, k_.ap[:, :], q_.ap[:, :], [k_, q_])
                            dve(lambda h: h.tensor_tensor(out=atT.ap[:, :], in0=p8.ap[:, 0:128], in1=decTU.ap[:, :], op=ALU.mult), [p8, decTU], [atT])
                            dve(lambda h: h.tensor_scalar(out=kd.ap[:, :], in0=ktr.ap[:, :], scalar1=cb.ap[:, 4:5], scalar2=None, op0=ALU.mult), [ktr, cb], [kd])
                            p9 = P()
                            mm(p9.ap[:, 0:128], p9, wTb.ap[:, :], Sst.ap[:, :], [wTb, Sst])
                            dve(lambda h: h.tensor_tensor(out=vnew.ap[:, :], in0=X.ap[:, 0:128], in1=p9.ap[:, 0:128], op=ALU.subtract), [X, p9], [vnew])
                            po = P()
                            mm(po.ap[:, 0:128], po, qdT.ap[:, :], Sst.ap[:, :], [qdT, Sst], start=True, stop=False)
                            mm(po.ap[:, 0:128], po, atT.ap[:, :], vnew.ap[:, :], [atT, vnew], start=False, stop=True)
                            pS = P()
                            mm(pS.ap[:, 0:128], pS, kd.ap[:, :], vnew.ap[:, :], [kd, vnew])
                            dve(lambda h: h.scalar_tensor_tensor(out=Sst.ap[:, :], in0=Sst.ap[:, :], scalar=cb.ap[:, 5:6], in1=pS.ap[:, 0:128], op0=ALU.mult, op1=ALU.add),
                                [Sst, cb, pS], [Sst])
                            if G2BSTOP and G2BSTOP <= 5:
                                continue
                            yield
                            act(sqo.ap[:, :], po.ap[:, 0:128], AF.Square, [po], [sqo])
                            dve(lambda h: h.tensor_reduce(out=cb.ap[:, 6:7], in_=sqo.ap[:, :], axis=AX.X, op=ALU.add), [sqo], [cb])
                            act(cb.ap[:, 6:7], cb.ap[:, 6:7], AF.Sqrt, [cb], [cb], scale=1.0 / 128, bias=EPS)
                            dve(lambda h: h.reciprocal(out=cb.ap[:, 6:7], in_=cb.ap[:, 6:7]), [cb], [cb])
                            dve(lambda h: h.scalar_tensor_tensor(out=og.ap[:, :], in0=po.ap[:, 0:128], scalar=cb.ap[:, 6:7], in1=onw, op0=ALU.mult, op1=ALU.mult), [po, cb, gpar], [og])
                            dve(lambda h: h.tensor_tensor(out=og.ap[:, :], in0=og.ap[:, :], in1=z_.ap[:, :], op=ALU.mult), [og, z_], [og])
                            pt_ = P()
                            S.op("pe", lambda h: h.transpose(pt_.ap[:, 0:128], og.ap[:, :], ident.ap[:, :]), reads=[og, ident], writes=[pt_])
                            ot_ = ogT[ci % 2]
                            act(ot_.ap[:, :], pt_.ap[:, 0:128], AF.Copy, [pt_], [ot_])
                            S.dma("pool", ot_.ch, OGT[hh, :, s0:s0 + nt], ot_.ap[:, 0:nt], reads=[ot_])
                for pair0 in range(0, HGC, 2):
                    if pair0 > 0:
                        ph.close()
                        ph = Phase()
                    gens_ = [head_gen(hh) for hh in range(pair0, min(HGC, pair0 + 2))]
                    while gens_:
                        for g_ in gens_[:]:
                            try:
                                next(g_)
                            except StopIteration:
                                gens_.remove(g_)
                ph.close()
                if os.environ.get("KDBG_STOP", "") == "g2b":
                    return None

                parts = rs_parts("gdn")
                ph = Phase()
                B = tp_bufs(ph, 8, wcols=HGC * 128)
                wo_pass(ph, B, f"wgo{j}", HGC, OGT, parts)
                ph.close()
                return reduce_scatter(parts)

            for layer in range(L):
                S.new_epoch(epoch_sems[layer])
                mix = None
                if layer in mla_layers:
                    mix = mla_layer(layer)
                if layer in gdn_layers:
                    mix = gdn_layer(layer)
                mlp_phase(layer, mix)

            ph = Phase()
            xin = [ph.buf("yin", [128, D], F32) for i in range(2)]
            xtr = [ph.buf("ytr", [128, KC, 128], F32) for i in range(2)]
            t_out = T(out_ext)
            for b in range(Tc // 128):
                xt, xb = xtr[b % 2], xin[b % 2]
                col0 = NM + b * 128
                S.dma("sp", xt.ch, xt.ap[:, :, :], hview[:, :, col0:col0 + 128], reads=[t_hT], writes=[xt])
                for kc in range(KC):
                    p = ps[kc % 4]
                    S.op("pe", lambda h: h.transpose(p.ap[:, 0:128], xt.ap[:, kc, :], ident.ap[:, :]), reads=[xt, ident], writes=[p])
                    if kc % 2 == 0:
                        S.op("act", lambda h: h.activation(out=xb.ap[:, kc * 128:(kc + 1) * 128], in_=p.ap[:, 0:128], func=AF.Copy),
                             reads=[p], writes=[xb])
                    else:
                        S.op("dve", lambda h: h.tensor_copy(out=xb.ap[:, kc * 128:(kc + 1) * 128], in_=p.ap[:, 0:128]),
                             reads=[p], writes=[xb])
                S.dma("pool", xb.ch, out_ext[b * 128:(b + 1) * 128, :], xb.ap[:, :], reads=[xb], writes=[t_out])
            ph.close()
        return nc


def rope_tables(T_):
    half = 32
    inv = (np.float32(10000.0) ** (-np.arange(0, 64, 2, dtype=np.float32) / np.float32(64))).astype(np.float32)
    ang = (np.arange(T_, dtype=np.float32)[:, None] * inv[None, :]).astype(np.float32)
    cos, sin = np.cos(ang).astype(np.float32), np.sin(ang).astype(np.float32)
    cosT = np.concatenate([cos.T, cos.T], axis=0)
    sinT = np.concatenate([-sin.T, sin.T], axis=0)
    return np.ascontiguousarray(cosT), np.ascontiguousarray(sinT)


def prep_inputs(cfg, inp, mixers="all"):
    c = cfg
    L = c.DEPTH
    f32 = np.float32
    x = np.asarray(inp["x"], f32).reshape(c.SEQ, c.D)
    gains = np.asarray(inp["norm_gains"], f32)
    g_fm = np.concatenate([fm_vec(gains[l, w]) for l in range(L) for w in range(4)], axis=1)
    KCTd = min(64, c.FF // 128)
    common = {"meta": np.asarray(inp["meta_tokens"], f32), "gains": g_fm, "ident": np.eye(128, dtype=f32)}
    per = [dict() for _ in range(NCORES)]
    for l in range(L):
        wu = tile_gathered(np.asarray(inp["mlp_w_up"][l], f32), c.KC)
        wd = tile_gathered(np.asarray(inp["mlp_w_down"][l], f32), KCTd)
        for r in range(NCORES):
            per[r][f"wup{l}"] = wu[r]
            per[r][f"wdn{l}"] = wd[r]
    if mixers in ("all", "mla"):
        nm = (L + 1) // 2
        cosT, sinT = rope_tables(c.T)
        common["ropecos"], common["ropesin"] = cosT, sinT
        common["tri"] = np.triu(np.ones((128, 128), f32))
        common["qkg"] = np.concatenate([fm_vec(np.concatenate([np.asarray(inp["mla_q_norm"][j], f32), np.asarray(inp["mla_kv_norm"][j], f32)])) for j in range(nm)], axis=1)
        sw = np.concatenate([np.arange(32, 64), np.arange(0, 32)])
        HMC, KVR, QR = c.HMC, c.KVR, c.QR
        for j in range(nm):
            wkva = np.asarray(inp["mla_wkv_a"][j], f32)
            pe = wkva[:, KVR:]
            wa = np.concatenate([np.asarray(inp["mla_wq_a"][j], f32), wkva[:, :KVR], pe, pe[:, sw]], axis=1)
            wag = tile_gathered(wa, c.KC)
            wqb = np.asarray(inp["mla_wq_b"][j], f32).reshape(QR, c.HM, 192)
            wkvb = np.asarray(inp["mla_wkv_b"][j], f32).reshape(KVR, c.HM, 256)
            wo = np.asarray(inp["mla_wo"][j], f32)
            for r in range(NCORES):
                per[r][f"wa{j}"] = wag[r]
                hs = slice(r * HMC, (r + 1) * HMC)
                q = wqb[:, hs, :]
                qcols = np.concatenate([np.concatenate([q[:, h, :128], q[:, h, 128:], q[:, h, 128:][:, sw]], axis=1) for h in range(HMC)], axis=1)
                per[r][f"wqb{j}"] = tile_local(qcols, 128)
                kk = wkvb[:, hs, :128].reshape(KVR, HMC * 128)
                per[r][f"wkk{j}"] = tile_local(kk, 128)
                vv = wkvb[:, hs, 128:].reshape(KVR, HMC * 128)
                per[r][f"wkv{j}"] = tile_local(vv, HMC * 128).reshape(-1, min(1024, c.KVR // 128 * HMC * 128))
                per[r][f"wol{j}"] = tile_local(wo[r * HMC * 128:(r + 1) * HMC * 128, :], 128)
    if mixers in ("all", "gdn"):
        ng = L // 2
        HGC, H = c.HGC, c.HG
        KW = H * 128
        ones = np.ones((128, 128), f32)
        common["gconst"] = np.concatenate([np.triu(ones), np.triu(ones, 1), np.tril(ones, -1), ones], axis=1)
        ii = np.arange(128)
        lms = []
        for lv in range(7):
            s_ = 1 << lv
            same = (ii[:, None] // (2 * s_)) == (ii[None, :] // (2 * s_))
            lms.append((same & ((ii[:, None] % (2 * s_)) >= s_) & ((ii[None, :] % (2 * s_)) < s_)).astype(f32))
        common["gmask"] = np.concatenate(lms + [m_.T for m_ in lms], axis=1)
        gpars = [[] for _ in range(NCORES)]
        for j in range(ng):
            wq = np.asarray(inp["gdn_w_qkvz"][j], f32)
            wba = np.asarray(inp["gdn_w_ba"][j], f32)
            cw = np.asarray(inp["gdn_conv_w"][j], f32)
            alog = np.asarray(inp["gdn_a_log"][j], f32)
            dtb = np.asarray(inp["gdn_dt_bias"][j], f32)
            onw = np.asarray(inp["gdn_o_norm"][j], f32)
            wo = np.asarray(inp["gdn_wo"][j], f32)
            for r in range(NCORES):
                cols, cwc = [], []
                for hh in range(HGC):
                    hs = r * HGC + hh
                    for ty in range(4):
                        base = ty * KW + hs * 128
                        cols.append(wq[:, base:base + 128])
                        if ty < 3:
                            cwc.append(cw[:, base:base + 128].T)
                per[r][f"wgq{j}"] = tile_local(np.concatenate(cols, axis=1), 128)
                hsl = np.arange(r * HGC, (r + 1) * HGC)
                per[r][f"wgba{j}"] = tile_local(np.concatenate([wba[:, hsl], wba[:, H + hsl]], axis=1), 2 * HGC)
                per[r][f"wgo{j}"] = tile_local(wo[r * HGC * 128:(r + 1) * HGC * 128, :], 128)
                gpars[r].append(np.concatenate(cwc + [np.broadcast_to(alog[hsl][None, :], (128, HGC)),
                                                      np.broadcast_to(dtb[hsl][None, :], (128, HGC)),
                                                      np.broadcast_to(onw[None, :], (128, 128))], axis=1))
        for r in range(NCORES):
            per[r]["gpar"] = np.concatenate(gpars[r], axis=1)
    maps = []
    for r in range(NCORES):
        m = dict(common)
        m["x"] = x[r * c.Tc:(r + 1) * c.Tc]
        m.update(per[r])
        maps.append(m)
    return maps


def run(cfg, inp, mixers="all", trace=False):
    b = Builder(cfg, mixers=mixers)
    nc = b.build()
    maps = prep_inputs(cfg, inp, mixers)
    maps = [{k: np.ascontiguousarray(m[k], dtype=np.float32).reshape(b.inputs[k]) for k in b.inputs} for m in maps]
    res = run_bass_kernel_spmd(nc, maps, core_ids=list(range(NCORES)), trace=trace)
    out = np.concatenate([res.results[r]["out"] for r in range(NCORES)], axis=0)
    return out.reshape(1, cfg.SEQ, cfg.D).astype(np.float32), res, b


def kernel(**inputs):
    cfg = Cfg()
    out, _, _ = run(cfg, inputs)
    return out
```
